# Optimizing a Trainium2 kernel written in Bass

```python
import jax
import jax.numpy as jnp
from jax import lax
import numpy as np

D_MODEL = 1024
BATCH = 8
SEQ = 4096
DEPTH = 2

MEM_LEN = 256
GRID_W = 64
EPS = 1e-6
NEG_INF = -1e30
HEAD_DIM = 64
D_FF = 11 * D_MODEL // 4
CONV_CH = D_MODEL // 4
CONV_WIDTH = 31
WIN_HEADS = (D_MODEL // 2) // HEAD_DIM
WIN_KV_HEADS = 2
WIN_GROUP = WIN_HEADS // WIN_KV_HEADS
WINDOW = 128
BLOCK = 128
T5_BUCKETS = 32
T5_MAX_DIST = 128
NA_HEADS = (D_MODEL // 4) // HEAD_DIM
NA_ROWS_MAX = 8
NA_COLS = 16
X_HEADS = 4
X_HEAD_DIM = D_MODEL // X_HEADS
N_BRANCH = 3
IN_WIDTHS = (2 * CONV_CH,
             WIN_HEADS * HEAD_DIM, WIN_KV_HEADS * HEAD_DIM, WIN_KV_HEADS * HEAD_DIM,
             NA_HEADS * HEAD_DIM, NA_HEADS * HEAD_DIM, NA_HEADS * HEAD_DIM,
             N_BRANCH * D_MODEL)
IN_WIDTH = sum(IN_WIDTHS)
IN_SPLITS = tuple(sum(IN_WIDTHS[:i + 1]) for i in range(len(IN_WIDTHS) - 1))

kernel_name = 'hybrid_conv_window_natten_encoder'


def rms_norm(x, g):
    xf = x.astype(jnp.float32)
    y = xf * lax.rsqrt(jnp.mean(xf * xf, axis=-1, keepdims=True) + EPS)
    return (y * g.astype(jnp.float32)).astype(x.dtype)


def layer_norm(x, g, b):
    xf = x.astype(jnp.float32)
    mu = jnp.mean(xf, axis=-1, keepdims=True)
    var = jnp.mean(jnp.square(xf - mu), axis=-1, keepdims=True)
    y = (xf - mu) * lax.rsqrt(var + EPS)
    return (y * g.astype(jnp.float32) + b.astype(jnp.float32)).astype(x.dtype)


def swiglu(x, w_gate, w_up, w_down):
    return (jax.nn.silu(x @ w_gate) * (x @ w_up)) @ w_down


def conformer_conv(u, dw_w, dw_b, ln_g, ln_b):
    a, gate = jnp.split(u, 2, axis=-1)
    z = a * jax.nn.sigmoid(gate)
    z = lax.conv_general_dilated(
        z, dw_w[:, None, :].astype(z.dtype), window_strides=(1,),
        padding=((CONV_WIDTH // 2, CONV_WIDTH // 2),),
        dimension_numbers=('NWC', 'WIO', 'NWC'),
        feature_group_count=CONV_CH) + dw_b
    z = layer_norm(z, ln_g, ln_b)
    return jax.nn.silu(z)


def t5_buckets(rel):
    half = T5_BUCKETS // 2
    max_exact = half // 2
    ret = (rel > 0).astype(np.int32) * half
    n = np.abs(rel)
    large = max_exact + (np.log(np.maximum(n, 1) / max_exact)
                         / np.log(T5_MAX_DIST / max_exact) * (half - max_exact)).astype(np.int32)
    large = np.minimum(large, half - 1)
    return ret + np.where(n < max_exact, n, large)


def windowed_gqa(q, k, v, sink, t5_table):
    bsz, seq = q.shape[0], q.shape[1]
    nblk = seq // BLOCK
    span = BLOCK + 2 * WINDOW
    rel = np.arange(span)[None, :] - WINDOW - np.arange(BLOCK)[:, None]
    band = np.abs(rel) <= WINDOW
    bias = jnp.transpose(t5_table[t5_buckets(rel)], (2, 0, 1)).astype(jnp.float32)
    bias = bias.reshape(WIN_KV_HEADS, WIN_GROUP, BLOCK, span)
    sink_g = sink.astype(jnp.float32).reshape(WIN_KV_HEADS, WIN_GROUP)[:, :, None, None]
    kp = jnp.pad(k, ((0, 0), (WINDOW, WINDOW), (0, 0), (0, 0)))
    vp = jnp.pad(v, ((0, 0), (WINDOW, WINDOW), (0, 0), (0, 0)))
    scale = HEAD_DIM ** -0.5

    def one_block(i):
        start = i * BLOCK
        qb = lax.dynamic_slice_in_dim(q, start, BLOCK, axis=1)
        kb = lax.dynamic_slice_in_dim(kp, start, span, axis=1)
        vb = lax.dynamic_slice_in_dim(vp, start, span, axis=1)
        s = jnp.einsum('bqkgd,bskd->bkgqs', qb, kb).astype(jnp.float32) * scale + bias
        kpos = start - WINDOW + jnp.arange(span)
        valid = jnp.logical_and(band, ((kpos >= 0) & (kpos < seq))[None, :])
        s = jnp.where(valid, s, NEG_INF)
        m = jnp.maximum(jnp.max(s, axis=-1, keepdims=True), sink_g)
        p = jnp.exp(s - m)
        p = p / (jnp.sum(p, axis=-1, keepdims=True) + jnp.exp(sink_g - m))
        o = jnp.einsum('bkgqs,bskd->bqkgd', p.astype(vb.dtype), vb)
        return o.reshape(bsz, BLOCK, WIN_HEADS * HEAD_DIM)

    out = lax.map(one_block, jnp.arange(nblk))
    return jnp.transpose(out, (1, 0, 2, 3)).reshape(bsz, seq, WIN_HEADS * HEAD_DIM)


def neighbourhood_attn(q, k, v, rpb):
    bsz, seq = q.shape[0], q.shape[1]
    rows = seq // GRID_W
    wr = min(NA_ROWS_MAX, rows)
    qg = q.reshape(bsz, rows, GRID_W, NA_HEADS, HEAD_DIM)
    kg = k.reshape(bsz, rows, GRID_W, NA_HEADS, HEAD_DIM)
    vg = v.reshape(bsz, rows, GRID_W, NA_HEADS, HEAD_DIM)
    col = np.arange(GRID_W)
    col_start = np.clip(col - NA_COLS // 2, 0, GRID_W - NA_COLS)
    col_idx = col_start[:, None] + np.arange(NA_COLS)[None, :]
    dcol = col_idx - col[:, None]
    rpb_c = rpb[:, :, dcol + NA_COLS - 1].astype(jnp.float32)
    scale = HEAD_DIM ** -0.5

    def one_row(r):
        rs = jnp.clip(r - wr // 2, 0, rows - wr)
        qr = lax.dynamic_index_in_dim(qg, r, axis=1, keepdims=False)
        kr = lax.dynamic_slice_in_dim(kg, rs, wr, axis=1)
        vr = lax.dynamic_slice_in_dim(vg, rs, wr, axis=1)
        kn = kr[:, :, col_idx]
        vn = vr[:, :, col_idx]
        drow = rs + jnp.arange(wr) - r
        bias = jnp.transpose(rpb_c[:, drow + NA_ROWS_MAX - 1], (0, 2, 1, 3))
        s = jnp.einsum('bchd,bwcjhd->bhcwj', qr, kn).astype(jnp.float32) * scale + bias
        p = jax.nn.softmax(s.reshape(bsz, NA_HEADS, GRID_W, wr * NA_COLS), axis=-1)
        p = p.reshape(bsz, NA_HEADS, GRID_W, wr, NA_COLS)
        o = jnp.einsum('bhcwj,bwcjhd->bchd', p.astype(vn.dtype), vn)
        return o.reshape(bsz, GRID_W, NA_HEADS * HEAD_DIM)

    out = lax.map(one_row, jnp.arange(rows))
    return jnp.transpose(out, (1, 0, 2, 3)).reshape(bsz, seq, NA_HEADS * HEAD_DIM)


def memory_cross_attn(h, mem_n, w_q, w_kv, w_o):
    bsz, seq = h.shape[0], h.shape[1]
    q = (h @ w_q).reshape(bsz, seq, X_HEADS, X_HEAD_DIM)
    k, v = jnp.split(mem_n @ w_kv, 2, axis=-1)
    k = k.reshape(bsz, -1, X_HEADS, X_HEAD_DIM)
    v = v.reshape(bsz, -1, X_HEADS, X_HEAD_DIM)
    s = jnp.einsum('bqhd,bmhd->bhqm', q, k).astype(jnp.float32) * (X_HEAD_DIM ** -0.5)
    p = jax.nn.softmax(s, axis=-1)
    o = jnp.einsum('bhqm,bmhd->bqhd', p.astype(v.dtype), v)
    return o.reshape(bsz, seq, X_HEADS * X_HEAD_DIM) @ w_o


def setup_inputs(seed: int = 0) -> dict:
    key = jax.random.key(seed)
    keys = iter(jax.random.split(key, 64))
    f32 = jnp.float32

    def w(shape, fan_in):
        return jax.random.normal(next(keys), shape, f32) * fan_in ** -0.5

    def gain(shape):
        return 1.0 + 0.02 * jax.random.normal(next(keys), shape, f32)

    def small(shape, s):
        return s * jax.random.normal(next(keys), shape, f32)

    L, D = DEPTH, D_MODEL
    return {
        'x': jax.random.normal(next(keys), (BATCH, SEQ, D), f32),
        'mem': jax.random.normal(next(keys), (BATCH, MEM_LEN, D), f32),
        'norm_ffn1': gain((L, D)),
        'ffn1_w_gate': w((L, D, D_FF), D),
        'ffn1_w_up': w((L, D, D_FF), D),
        'ffn1_w_down': w((L, D_FF, D), D_FF),
        'norm_mix': gain((L, D)),
        'w_in': w((L, D, IN_WIDTH), D),
        'conv_dw_w': w((L, CONV_WIDTH, CONV_CH), CONV_WIDTH),
        'conv_dw_b': small((L, CONV_CH), 0.02),
        'conv_ln_g': gain((L, CONV_CH)),
        'conv_ln_b': small((L, CONV_CH), 0.02),
        'conv_w_out': w((L, CONV_CH, D), CONV_CH),
        'win_sink': small((L, WIN_HEADS), 0.5),
        't5_bias': small((T5_BUCKETS, WIN_HEADS), 0.1),
        'win_w_out': w((L, WIN_HEADS * HEAD_DIM, D), WIN_HEADS * HEAD_DIM),
        'na_rpb': small((L, NA_HEADS, 2 * NA_ROWS_MAX - 1, 2 * NA_COLS - 1), 0.1),
        'na_w_out': w((L, NA_HEADS * HEAD_DIM, D), NA_HEADS * HEAD_DIM),
        'w_out': w((L, D, D), D),
        'norm_cross': gain((L, D)),
        'norm_mem': gain((L, D)),
        'cross_w_q': w((L, D, X_HEADS * X_HEAD_DIM), D),
        'cross_w_kv': w((L, D, 2 * X_HEADS * X_HEAD_DIM), D),
        'cross_w_o': w((L, X_HEADS * X_HEAD_DIM, D), X_HEADS * X_HEAD_DIM),
        'norm_ffn2': gain((L, D)),
        'ffn2_w_gate': w((L, D, D_FF), D),
        'ffn2_w_up': w((L, D, D_FF), D),
        'ffn2_w_down': w((L, D_FF, D), D_FF),
        'norm_final': gain((D,)),
    }


def reference(x, mem, norm_ffn1, ffn1_w_gate, ffn1_w_up, ffn1_w_down, norm_mix, w_in,
              conv_dw_w, conv_dw_b, conv_ln_g, conv_ln_b, conv_w_out, win_sink, t5_bias,
              win_w_out, na_rpb, na_w_out, w_out, norm_cross, norm_mem, cross_w_q,
              cross_w_kv, cross_w_o, norm_ffn2, ffn2_w_gate, ffn2_w_up, ffn2_w_down,
              norm_final):
    bsz, seq, _ = x.shape
    for l in range(DEPTH):
        x = x + 0.5 * swiglu(rms_norm(x, norm_ffn1[l]), ffn1_w_gate[l], ffn1_w_up[l], ffn1_w_down[l])
        h = rms_norm(x, norm_mix[l])
        u_a, bq, bk, bv, cq, ck, cv, gates = jnp.split(h @ w_in[l], IN_SPLITS, axis=-1)
        y_a = conformer_conv(u_a, conv_dw_w[l], conv_dw_b[l], conv_ln_g[l], conv_ln_b[l]) @ conv_w_out[l]
        y_b = windowed_gqa(
            bq.reshape(bsz, seq, WIN_KV_HEADS, WIN_GROUP, HEAD_DIM),
            bk.reshape(bsz, seq, WIN_KV_HEADS, HEAD_DIM),
            bv.reshape(bsz, seq, WIN_KV_HEADS, HEAD_DIM),
            win_sink[l], t5_bias) @ win_w_out[l]
        y_c = neighbourhood_attn(
            cq.reshape(bsz, seq, NA_HEADS, HEAD_DIM),
            ck.reshape(bsz, seq, NA_HEADS, HEAD_DIM),
            cv.reshape(bsz, seq, NA_HEADS, HEAD_DIM),
            na_rpb[l]) @ na_w_out[l]
        g = jax.nn.sigmoid(gates).reshape(bsz, seq, N_BRANCH, D_MODEL)
        y = g[:, :, 0] * y_a + g[:, :, 1] * y_b + g[:, :, 2] * y_c
        x = x + y @ w_out[l]
        x = x + memory_cross_attn(rms_norm(x, norm_cross[l]), rms_norm(mem, norm_mem[l]),
                                  cross_w_q[l], cross_w_kv[l], cross_w_o[l])
        x = x + 0.5 * swiglu(rms_norm(x, norm_ffn2[l]), ffn2_w_gate[l], ffn2_w_up[l], ffn2_w_down[l])
    return rms_norm(x, norm_final)
```

```python
import contextlib
import numpy as np
import concourse.bass as bass
import concourse.mybir as mybir
from concourse.bass_utils import run_bass_kernel_spmd

F32 = mybir.dt.float32
BF16 = mybir.dt.bfloat16
AF = mybir.ActivationFunctionType
ALU = mybir.AluOpType
AX = mybir.AxisListType

D = 1024
S = 4096
L = 2
T = 512
NT = S // T
DFF = 2816
NFF = DFF // 128
MEM = 256
EPS = 1e-6
NEG = -30000.0

EPOCH = 8192
RING = {"sp": 28, "pool": 16, "act": 8, "conv": 2, "conv0": 6}


class Op:
    __slots__ = ("eng", "fn", "reads", "writes", "dma", "deps", "sig", "sigval", "slot", "target", "idx", "ring")

    def __init__(self, eng, fn, reads, writes, dma):
        self.eng = eng
        self.fn = fn
        self.reads = reads
        self.writes = writes
        self.dma = dma
        self.deps = []
        self.sig = False
        self.sigval = 0
        self.slot = -1
        self.target = 0


class Tile:
    __slots__ = ("name", "buf", "nsub", "t")

    def __init__(self, t, name, buf, nsub):
        self.t = t
        self.name = name
        self.buf = buf
        self.nsub = nsub

    def k(self, i=None):
        if i is None:
            return [(self.name, self.buf, j) for j in range(self.nsub)]
        return [(self.name, self.buf, i)]

    def __getitem__(self, idx):
        return self.t[idx]


class View:
    def __init__(self, t, keyfn, n):
        self.t = t
        self.keyfn = keyfn
        self.n = n

    def k(self, i=None):
        if i is None:
            return [k for j in range(self.n) for k in self.keyfn(j)]
        return list(self.keyfn(i))

    def __getitem__(self, idx):
        return self.t[idx]


class Prog:
    def __init__(self, nc):
        self.nc = nc
        self.ops = []
        self.last_write = {}
        self.readers = {}
        self.stack = contextlib.ExitStack()
        self.ndma = {}
        self.dma_ops = {}
        self.pools = {}

    def sbuf(self, name, shape, dtype):
        return self.stack.enter_context(self.nc.sbuf_tensor("sb_" + name, list(shape), dtype))

    def psum(self, name, shape, dtype):
        return self.stack.enter_context(self.nc.psum_tensor("ps_" + name, list(shape), dtype))

    def pool(self, name, shape, dtype, bufs, nsub=1):
        tiles = []
        for b in range(bufs):
            t = self.sbuf(f"{name}_{b}", shape, dtype)
            tiles.append(Tile(t, name, b, nsub))
        self.pools[name] = [tiles, 0]
        return name

    def get(self, name):
        tiles, i = self.pools[name]
        self.pools[name][1] = i + 1
        return tiles[i % len(tiles)]

    def add(self, eng, fn, reads=(), writes=(), dma=False, ring=None):
        op = Op(eng, fn, list(reads), list(writes), dma)
        op.ring = ring or eng
        j = len(self.ops)
        op.idx = j
        deps = set()
        for k in op.reads:
            w = self.last_write.get(k)
            if w is not None:
                deps.add(w)
        for k in op.writes:
            w = self.last_write.get(k)
            if w is not None:
                deps.add(w)
            for r in self.readers.get(k, ()):
                deps.add(r)
        for k in op.writes:
            self.last_write[k] = j
            self.readers[k] = []
        ws = set(op.writes)
        for k in op.reads:
            if k not in ws:
                self.readers.setdefault(k, []).append(j)
        if dma:
            rn = op.ring
            n = self.ndma.get(rn, 0)
            ring = RING[rn]
            op.slot = n % ring
            op.target = 16 * (n // ring + 1)
            lst = self.dma_ops.setdefault(rn, [])
            if n >= ring:
                deps.add(lst[n - ring])
            lst.append(j)
            self.ndma[rn] = n + 1
            op.sig = True
        deps.discard(j)
        op.deps = sorted(deps)
        self.ops.append(op)
        return j

    def pe(self, fn, reads=(), writes=()):
        return self.add("pe", fn, reads, writes)

    def act(self, fn, reads=(), writes=()):
        return self.add("act", fn, reads, writes)

    def dve(self, fn, reads=(), writes=()):
        return self.add("dve", fn, reads, writes)

    def dma(self, eng, out, in_, reads=(), writes=(), ring=None):
        return self.add(eng, lambda e: e.dma_start(out=out, in_=in_), reads, writes, dma=True, ring=ring)

    def emit(self):
        nc = self.nc
        ops = self.ops

        def skip(p, op):
            return p.eng == "pe" and op.eng == "pe" and not op.dma and not p.dma

        for op in ops:
            for d in op.deps:
                p = ops[d]
                if p.dma or skip(p, op):
                    continue
                p.sig = True
        cnt = {}
        for op in ops:
            if op.dma or not op.sig:
                continue
            c = cnt.get(op.eng, 0)
            op.slot = c // EPOCH
            op.sigval = c % EPOCH + 1
            cnt[op.eng] = c + 1
        sems = {}
        for eng, c in cnt.items():
            for e in range((c + EPOCH - 1) // EPOCH):
                sems[(eng, e)] = self.stack.enter_context(nc.semaphore(f"s_{eng}_{e}"))
        for eng, n in self.ndma.items():
            for r in range(min(RING[eng], n)):
                sems[("dma" + eng, r)] = self.stack.enter_context(nc.semaphore(f"s_dma{eng}_{r}"))

        by_eng = {}
        for op in ops:
            by_eng.setdefault(op.eng, []).append(op)

        def run_engine(engname, e):
            known = {}
            for op in by_eng.get(engname, []):
                need = {}
                for d in op.deps:
                    p = ops[d]
                    if p.dma:
                        key = ("dma" + p.ring, p.slot)
                        val = p.target
                    else:
                        if skip(p, op):
                            continue
                        key = (p.eng, p.slot)
                        val = p.sigval
                    if known.get(key, 0) >= val:
                        continue
                    if need.get(key, 0) < val:
                        need[key] = val
                for key, val in need.items():
                    e.wait_ge(sems[key], val)
                    known[key] = val
                ins = op.fn(e)
                if op.dma:
                    ins.then_inc(sems[("dma" + op.ring, op.slot)], 16)
                elif op.sig:
                    ins.then_inc(sems[(op.eng, op.slot)], 1)

        with nc.Block() as block:
            @block.tensor
            def _(e):
                run_engine("pe", e)

            @block.scalar
            def _(e):
                run_engine("act", e)

            @block.vector
            def _(e):
                run_engine("dve", e)

            @block.gpsimd
            def _(e):
                run_engine("pool", e)

            @block.sync
            def _(e):
                run_engine("sp", e)

    def close(self):
        self.stack.close()


def opt_b(w):
    K, F = w.shape
    kc, fc = K // 128, F // 128
    return np.ascontiguousarray(w.reshape(kc, 128, fc, 128).transpose(2, 1, 0, 3).reshape(fc, 128, kc * 128))


def opt_a(w):
    K, N = w.shape
    kc = K // 128
    return np.ascontiguousarray(w.reshape(kc, 128, N).transpose(1, 0, 2).reshape(128, kc * N))


def cols(v):
    return np.ascontiguousarray(v.reshape(-1, 128).T)


def t5_buckets(rel):
    half = 16
    max_exact = 8
    ret = (rel > 0).astype(np.int32) * half
    n = np.abs(rel)
    large = max_exact + (np.log(np.maximum(n, 1) / max_exact) / np.log(128 / max_exact) * (half - max_exact)).astype(np.int32)
    large = np.minimum(large, half - 1)
    return ret + np.where(n < max_exact, n, large)


def na_geometry():
    reps = [0, 1, 2, 30, 31]
    drow = np.zeros((5, 128, 576), np.int64)
    dcol = np.zeros((5, 128, 576), np.int64)
    valid = np.zeros((5, 128, 576), bool)
    for v, j in enumerate(reps):
        kb = min(max(2 * j - 4, 0), 55)
        for q in range(128):
            r = 2 * j + q // 64
            c = q % 64
            rs = min(max(r - 4, 0), 56)
            cs = min(max(c - 8, 0), 48)
            kr = kb + np.arange(576) // 64
            kc = np.arange(576) % 64
            ok = (kr >= rs) & (kr < rs + 8) & (kc >= cs) & (kc < cs + 16)
            valid[v, q] = ok
            drow[v, q] = np.where(ok, kr - r + 7, 0)
            dcol[v, q] = np.where(ok, kc - c + 15, 0)
    return drow, dcol, valid


def na_variant(j):
    if j <= 1:
        return j
    if j >= 30:
        return j - 27
    return 2


WNAMES = ["gu1", "wd1", "winab", "winaa", "winb", "cwo", "wwo", "nwo", "wo", "xq", "xk", "xv", "xo", "gu2", "wd2"]
QPERM = [0, 4, 1, 5, 2, 6, 3, 7]


def prep_weights(inp):
    out = {}
    for l in range(L):
        for n, tag in ((1, "ffn1"), (2, "ffn2")):
            g = opt_b(inp[f"{tag}_w_gate"][l])
            u = opt_b(inp[f"{tag}_w_up"][l])
            out[f"gu{n}_{l}"] = np.ascontiguousarray(np.concatenate([g, u], axis=2))
            out[f"wd{n}_{l}"] = opt_b(inp[f"{tag}_w_down"][l])
        w = inp["w_in"][l]
        ua, bq, bk, bv, cq, ck, cv, gt = np.split(w, [512, 1024, 1152, 1280, 1536, 1792, 2048], axis=1)
        a, gate = ua[:, :256], ua[:, 256:]
        out[f"winab_{l}"] = opt_b(np.concatenate([gate, a, bk, ck], axis=1))
        out[f"winaa_{l}"] = opt_a(np.concatenate([bv, cv], axis=1))
        bqp = bq.reshape(D, 8, 64)[:, QPERM, :].reshape(D, 512)
        out[f"winb_{l}"] = opt_b(np.concatenate([bqp, cq, gt], axis=1))
        out[f"cwo_{l}"] = opt_b(inp["conv_w_out"][l])
        wwo = inp["win_w_out"][l].reshape(8, 64, D)[QPERM].reshape(512, D)
        out[f"wwo_{l}"] = opt_b(wwo)
        out[f"nwo_{l}"] = opt_b(inp["na_w_out"][l])
        out[f"wo_{l}"] = opt_b(inp["w_out"][l])
        out[f"xq_{l}"] = opt_b(inp["cross_w_q"][l])
        out[f"xk_{l}"] = opt_b(inp["cross_w_kv"][l][:, :D])
        out[f"xv_{l}"] = opt_a(inp["cross_w_kv"][l][:, D:])
        out[f"xo_{l}"] = opt_b(inp["cross_w_o"][l])
    vec = []
    for l in range(L):
        for nm in ("norm_ffn1", "norm_mix", "norm_cross", "norm_mem", "norm_ffn2"):
            vec.append(cols(inp[nm][l]))
    vec.append(cols(inp["norm_final"]))
    for l in range(L):
        dw = inp["conv_dw_w"][l]
        vec.append(np.ascontiguousarray(dw.T.reshape(2, 128, 31).transpose(1, 0, 2).reshape(128, 62)))
    for l in range(L):
        vec.append(cols(inp["conv_dw_b"][l]))
        vec.append(cols(inp["conv_ln_g"][l]))
        vec.append(cols(inp["conv_ln_b"][l]))
    for l in range(L):
        sk = inp["win_sink"][l][QPERM]
        vec.append(np.ascontiguousarray(np.broadcast_to(sk[None, :], (128, 8))))
    out["vecs"] = np.ascontiguousarray(np.concatenate(vec, axis=1).astype(np.float32))
    out["ident"] = np.eye(128, dtype=np.float32)
    rel = np.arange(384)[None, :] - 128 - np.arange(128)[:, None]
    band = np.abs(rel) <= 128
    tb = inp["t5_bias"][t5_buckets(rel)]
    tb = np.where(band[:, :, None], tb, np.float32(NEG)).transpose(0, 2, 1)[:, QPERM, :]
    out["t5tab"] = np.ascontiguousarray(tb.reshape(128, 8 * 384).astype(np.float32))
    cm = np.zeros((1, 2, 384), np.float32)
    cm[0, 0, :128] = NEG
    cm[0, 1, 256:] = NEG
    out["colmask"] = cm.reshape(1, 768)
    drow, dcol, valid = na_geometry()
    nat = np.zeros((L, 5, 128, 4, 576), np.float32)
    for l in range(L):
        for h in range(4):
            g = inp["na_rpb"][l][h][drow, dcol]
            nat[l, :, :, h, :] = np.where(valid, g, np.float32(NEG))
    out["natab"] = np.ascontiguousarray(nat.reshape(L, 5, 128, 4 * 576))
    return out


VOFF = {}


def _vec_offsets():
    o = 0
    for l in range(L):
        for nm in ("ffn1", "mix", "cross", "mem", "ffn2"):
            VOFF[(nm, l)] = o
            o += 8
    VOFF["final"] = o
    o += 8
    for l in range(L):
        VOFF[("dw", l)] = o
        o += 62
    for l in range(L):
        VOFF[("dwb", l)] = o
        o += 2
        VOFF[("lng", l)] = o
        o += 2
        VOFF[("lnb", l)] = o
        o += 2
    for l in range(L):
        VOFF[("sink", l)] = o
        o += 8
    return o


NVEC = _vec_offsets()

WSHAPES = {
    "gu1": [NFF, 128, 2048], "gu2": [NFF, 128, 2048], "wd1": [8, 128, DFF], "wd2": [8, 128, DFF],
    "winab": [7, 128, 1024], "winaa": [128, 8 * 384], "winb": [30, 128, 1024],
    "cwo": [8, 128, 256], "wwo": [8, 128, 512], "nwo": [8, 128, 256], "wo": [8, 128, 1024],
    "xq": [8, 128, 1024], "xk": [8, 128, 1024], "xv": [128, 8 * 1024], "xo": [8, 128, 1024],
}
WGROUP = {"gu1": 2, "gu2": 2, "wd1": 1, "wd2": 1, "winab": 4, "winb": 4, "cwo": 8, "wwo": 8, "nwo": 8,
          "wo": 4, "xq": 4, "xk": 4, "xo": 4}


def build_program(stage=99, debug_out=False):
    nc = bass.Bass("TRN2", target_bir_lowering=False)
    P = Prog(nc)
    dt = {}

    def din(name, shape):
        dt[name] = nc.dram_tensor(name, list(shape), F32, kind="ExternalInput").ap()
        return dt[name]

    xT_in = din("xT", [D, S])
    memT_in = din("memT", [D, MEM])
    vecs_in = din("vecs", [128, NVEC])
    ident_in = din("ident", [128, 128])
    t5_in = din("t5tab", [128, 8 * 384])
    cmask_in = din("colmask", [1, 768])
    natab_in = din("natab", [L, 5, 128, 4 * 576])
    wf = {}
    wb = {}
    for l in range(L):
        for n in WNAMES:
            wf[(n, l)] = din(f"{n}_{l}", WSHAPES[n])
            wb[(n, l)] = nc.dram_tensor(f"b_{n}_{l}", WSHAPES[n], BF16).ap()
    natb = nc.dram_tensor("natb", [L, 5, 128, 4 * 576], BF16).ap()
    outT = nc.dram_tensor("outT", [D, S], F32, kind="ExternalOutput").ap()
    xs = nc.dram_tensor("xs", [D, S], F32).ap()
    hTs = [nc.dram_tensor(f"hTs{l}", [D, S], BF16).ap() for l in range(L)]
    zs = [nc.dram_tensor(f"zs{l}", [256, S], BF16).ap() for l in range(L)]
    kTs = [nc.dram_tensor(f"kTs{l}", [128, S], BF16).ap() for l in range(L)]
    vs = [nc.dram_tensor(f"vs{l}", [S, 128], BF16).ap() for l in range(L)]
    ckTs = [nc.dram_tensor(f"ckTs{l}", [256, S], BF16).ap() for l in range(L)]
    cvs = [nc.dram_tensor(f"cvs{l}", [S, 512], BF16).ap() for l in range(L)]

    vecs = P.sbuf("vecs", [128, NVEC], F32)
    identf = P.sbuf("identf", [128, 128], F32)
    identb = P.sbuf("identb", [128, 128], BF16)
    onesb = P.sbuf("onesb", [128, 128], BF16)
    t5tab = P.sbuf("t5tab", [128, 8, 384], BF16)
    cmask = P.sbuf("cmask", [1, 2, 384], BF16)
    natreg = P.sbuf("natreg", [128, 4, 576], BF16)
    KmT = P.sbuf("KmT", [128, 8, MEM], BF16)
    Vm = P.sbuf("Vm", [128, 2, D], BF16)
    diag = P.sbuf("diag", [128, 62, 128], BF16)
    P.pool("xT", [128, 8, T], F32, 2, nsub=8)
    P.pool("hT", [128, 8, T], BF16, 1, nsub=8)
    P.pool("w", [128, 4096], BF16, 3)
    P.pool("sq", [128, T], BF16, 3)
    P.pool("st", [128, T], F32, 4)
    P.pool("tg", [128, 8, T], BF16, 1, nsub=8)
    P.pool("bq", [128, 4, T], BF16, 1, nsub=4)
    P.pool("cq", [128, 2, T], BF16, 1, nsub=2)
    P.pool("kh", [128, 768], BF16, 1)
    P.pool("vlo", [128, 6, 128], BF16, 1)
    P.pool("vhi", [128, 6, 128], BF16, 1)
    P.pool("ckh", [128, 2, 576], BF16, 2)
    P.pool("cvp", [128, 5, 4, 128], BF16, 2)
    P.pool("nate", [128, 4, 576], BF16, 2)
    P.pool("pp", [128, 1024], BF16, 2, nsub=2)
    P.pool("ptb", [128, 1024], BF16, 2, nsub=2)
    P.pool("sm", [128, 32], F32, 6)
    P.pool("yb", [128, 5, T], BF16, 1, nsub=5)
    P.pool("zh", [128, 2, T + 30], BF16, 1)
    P.pool("cvf", [128, 2, T], F32, 1, nsub=2)
    P.pool("cvb", [128, 4, T], BF16, 1)
    P.pool("vo", [128, 4, 128 + 512], BF16, 1)
    big = P.sbuf("big", [128, 24 * T], BF16)
    actT_v = View(big[:, 0:NFF * T].rearrange("p (c t) -> p c t", t=T), lambda c: [("big", c)], NFF)
    acc_v = View(big[:, 0:16 * T].bitcast(F32).rearrange("p (c t) -> p c t", t=T), lambda c: [("big", 2 * c), ("big", 2 * c + 1)], 8)
    ybf_v = View(big[:, 16 * T:24 * T].rearrange("p (c t) -> p c t", t=T), lambda c: [("big", 16 + c)], 8)
    views = {"actT": actT_v, "acc": acc_v, "ybf": ybf_v}
    _get = P.get

    def get(name):
        if name in views:
            return views[name]
        return _get(name)
    P.get = get
    psb = [P.psum(f"psb{i}", [128, 2, 512], F32) for i in range(4)]

    pj_state = {"i": 0, "set": list(range(8))}

    def pj():
        s = pj_state["set"]
        h = s[pj_state["i"] % len(s)]
        pj_state["i"] += 1
        return psb[h // 2][:, h % 2, :], [("ps", h)]

    sc_state = {"i": 0}

    def sc2():
        b = sc_state["i"] % 2
        sc_state["i"] += 1
        return psb[b], [("ps", 2 * b), ("ps", 2 * b + 1)]

    def pt2():
        return psb[2], [("ps", 4), ("ps", 5)]

    def vcol(key, c0=0, n=1):
        o = VOFF[key] + c0
        return vecs[:, o:o + n]

    ev = {"i": 0}

    def alt():
        ev["i"] += 1
        return "act" if ev["i"] % 2 else "dve"

    def copy_to(eng, out, in_, reads, writes, scale=None):
        if eng == "act":
            if scale is None:
                P.act(lambda e: e.activation(out=out, in_=in_, func=AF.Copy), reads, writes)
            else:
                P.act(lambda e: e.activation(out=out, in_=in_, func=AF.Copy, scale=scale), reads, writes)
        else:
            if scale is None:
                P.dve(lambda e: e.tensor_copy(out=out, in_=in_), reads, writes)
            else:
                P.dve(lambda e: e.tensor_scalar(out=out, in0=in_, scalar1=scale, scalar2=None, op0=ALU.mult), reads, writes)

    epsc = P.sbuf("epsc", [128, 1], F32)
    P.dve(lambda e: e.memset(epsc[:], EPS), [], ["epsc"])
    dummy = P.sbuf("dummy", [128, 4], F32)
    P.dve(lambda e: e.memset(dummy[:, 2:3], 1.0), [], ["dummy"])

    P.dma("sp", vecs[:], vecs_in, writes=["vecs"])
    sinkb = P.sbuf("sinkb", [128, 8 * L], BF16)
    P.dve(lambda e: e.tensor_copy(out=sinkb[:], in_=vecs[:, VOFF[("sink", 0)]:VOFF[("sink", 0)] + 8 * L]), ["vecs"], ["sinkb"])
    P.dma("sp", identf[:], ident_in, writes=["identf"])
    P.dve(lambda e: e.tensor_copy(out=identb[:], in_=identf[:]), ["identf"], ["identb"])
    P.dve(lambda e: e.memset(onesb[:], 1.0), [], ["onesb"])

    for nm in ("vlo", "vhi", "vo"):
        for tl_ in P.pools[nm][0]:
            P.dve(lambda e, tl_=tl_: e.memset(tl_[:], 0.0), [], tl_.k())

    def convert_items(n, l):
        src, dst = wf[(n, l)], wb[(n, l)]
        items = []
        if n in ("winaa", "xv"):
            ncol = WSHAPES[n][1]
            step = 2048
            for c0 in range(0, ncol, step):
                c1 = min(ncol, c0 + step)
                wk = [("wb", n, l, "all")] if c0 == 0 else [("wbx", n, l, c0)]
                items.append(lambda gate, c0=c0, c1=c1, wk=wk, ring="conv": P.dma("pool", dst[:, c0:c1], src[:, c0:c1], reads=gate, writes=wk, ring=ring))
            return items
        g = WGROUP[n]
        nfc = WSHAPES[n][0]
        for f0 in range(0, nfc, g):
            f1 = min(nfc, f0 + g)
            items.append(lambda gate, f0=f0, f1=f1, ring="conv": P.dma("pool", dst[f0:f1].rearrange("g p k -> p g k"), src[f0:f1].rearrange("g p k -> p g k"),
                                                                       reads=gate, writes=[("wb", n, l, f0 // g)], ring=ring))
        return items

    def convert(n, l):
        for it in convert_items(n, l):
            it([], ring="conv0")

    tick_state = {"n": 0}

    def tick():
        k = ("tick", tick_state["n"])
        tick_state["n"] += 1
        P.dve(lambda e: e.memset(dummy[:, 0:1], 1.0), [], [k, "tickmem"])
        return [k]

    def wkeys_flat(n, l):
        ncol = WSHAPES[n][1]
        return [("wb", n, l, "all")] + [("wbx", n, l, c0) for c0 in range(2048, ncol, 2048)]

    def stream_proj(n, l, fcs, KC, rhs_fn, rhs_keys, N, evac):
        g = WGROUP[n]
        per = KC * 128
        groups = {}
        for fc in fcs:
            groups.setdefault(fc // g, []).append(fc)
        for gi, lst in groups.items():
            wt = P.get("w")
            f0 = gi * g
            f1 = min(WSHAPES[n][0], f0 + g)
            nf = f1 - f0
            wsl = wb[(n, l)][f0:f1].rearrange("g p k -> p g k")
            P.dma("sp", wt[:, 0:nf * per].rearrange("p (g k) -> p g k", g=nf), wsl,
                  reads=[("wb", n, l, gi)], writes=wt.k())
            for fc in lst:
                base = (fc - f0) * per
                ps, psk = pj()
                for kc in range(KC):
                    P.pe(lambda e, ps=ps, wt=wt, o=base + kc * 128, kc=kc: e.matmul(
                        ps[:, 0:N], lhsT=wt[:, o:o + 128], rhs=rhs_fn(kc), start=(kc == 0), stop=(kc == KC - 1)),
                        reads=wt.k() + rhs_keys(kc), writes=psk)
                evac(fc, ps, psk)

    STATE = {"pool_ok": False}
    HOOKS = {"conv_point": lambda: None}

    def rstd_from(ps, psk, n_feat, N):
        ms = P.get("st")
        P.act(lambda e: e.activation(out=ms[:, 0:N], in_=ps[:, 0:N], func=AF.Ln, bias=epsc[:, 0:1], scale=1.0 / n_feat), psk + ["epsc"], ms.k())
        P.act(lambda e: e.activation(out=ms[:, 0:N], in_=ms[:, 0:N], func=AF.Exp, scale=-0.5), ms.k(), ms.k())
        return ms

    def rmsnorm(src, src_keys, gkey, dst, dst_keys, N, out_f32=None):
        ps, psk = pj()
        for c in range(8):
            sq = P.get("sq")
            P.act(lambda e, c=c, sq=sq: e.activation(out=sq[:, 0:N], in_=src(c), func=AF.Square), src_keys(c), sq.k())
            P.pe(lambda e, c=c, sq=sq: e.matmul(ps[:, 0:N], lhsT=onesb[:], rhs=sq[:, 0:N], start=(c == 0), stop=(c == 7)),
                 sq.k() + ["onesb"], psk)
        P.act(lambda e: e.activation(out=dummy[:, 3:4], in_=dummy[:, 2:3], func=AF.Ln), ["dummy"], ["dummy2"])
        rs = rstd_from(ps, psk, D, N)
        for c in range(8):
            if STATE["pool_ok"] and c % 2 == 1:
                eng = lambda fn, r, w: P.add("pool", fn, r, w)
            else:
                eng = P.dve
            eng(lambda e, c=c: e.scalar_tensor_tensor(out=dst(c), in0=src(c), scalar=vcol(gkey, c), in1=rs[:, 0:N],
                                                      op0=ALU.mult, op1=ALU.mult),
                src_keys(c) + rs.k() + ["vecs"], dst_keys(c))

    def ffn(l, which, xt):
        hT = P.get("hT")
        rmsnorm(lambda c: xt[:, c, :], lambda c: xt.k(c), ("ffn%d" % which, l), lambda c: hT[:, c, :], lambda c: hT.k(c), T)
        aT = P.get("actT")
        gun = "gu%d" % which
        g = WGROUP[gun]
        for f0 in range(0, NFF, g):
            wt = P.get("w")
            wsl = wb[(gun, l)][f0:f0 + g].rearrange("g p k -> p g k")
            P.dma("sp", wt[:, 0:g * 2048].rearrange("p (g k) -> p g k", g=g), wsl, reads=[("wb", gun, l, f0 // g)], writes=wt.k())
            for fi in range(g):
                fc = f0 + fi
                psg, pgk = pj()
                psu, puk = pj()
                for kc in range(8):
                    P.pe(lambda e, o=fi * 2048 + kc * 128, kc=kc, wt=wt, psg=psg: e.matmul(
                        psg, lhsT=wt[:, o:o + 128], rhs=hT[:, kc, :], start=(kc == 0), stop=(kc == 7)),
                        wt.k() + hT.k(kc), pgk)
                for kc in range(8):
                    P.pe(lambda e, o=fi * 2048 + 1024 + kc * 128, kc=kc, wt=wt, psu=psu: e.matmul(
                        psu, lhsT=wt[:, o:o + 128], rhs=hT[:, kc, :], start=(kc == 0), stop=(kc == 7)),
                        wt.k() + hT.k(kc), puk)
                tt = P.get("st")
                P.act(lambda e, tt=tt, psg=psg: e.activation(out=tt[:], in_=psg, func=AF.Tanh, scale=0.5), pgk, tt.k())
                P.dve(lambda e, tt=tt, psg=psg: e.scalar_tensor_tensor(out=tt[:], in0=tt[:], scalar=1.0, in1=psg,
                                                                         op0=ALU.add, op1=ALU.mult), tt.k() + pgk, tt.k())
                P.dve(lambda e, tt=tt, psu=psu, fc=fc: e.tensor_tensor(out=aT[:, fc, :], in0=tt[:], in1=psu, op=ALU.mult),
                      tt.k() + puk, aT.k(fc))
            HOOKS["conv_point"]()
        wdn = "wd%d" % which

        def evac(dc, ps, psk):
            P.dve(lambda e: e.scalar_tensor_tensor(out=xt[:, dc, :], in0=ps, scalar=0.25, in1=xt[:, dc, :],
                                                   op0=ALU.mult, op1=ALU.add), psk + xt.k(dc), xt.k(dc))
            HOOKS["conv_point"]()

        stream_proj(wdn, l, list(range(8)), NFF, lambda kc: aT[:, kc, :], lambda kc: aT.k(kc), T, evac)

    def tile_cols(i):
        return slice(i * T, (i + 1) * T)

    def fm(ap, i):
        return ap.rearrange("(c p) t -> p c t", p=128)[:, :, tile_cols(i)]

    def pass_a(l, i, xt, hook=None):
        ffn(l, 1, xt)
        wva = P.get("w")
        P.dma("sp", wva[:, 0:8 * 384], wb[("winaa", l)], reads=wkeys_flat("winaa", l), writes=wva.k())
        hT = P.get("hT")
        rmsnorm(lambda c: xt[:, c, :], lambda c: xt.k(c), ("mix", l), lambda c: hT[:, c, :], lambda c: hT.k(c), T)
        P.dma("pool", fm(hTs[l], i), hT[:], reads=hT.k(), writes=[("hTs", l, i)])
        zo = P.get("yb")
        tgl = [None, None]

        def evac(fc, ps, psk):
            if fc < 2:
                tt = P.get("st")
                tgl[fc] = tt
                P.act(lambda e: e.activation(out=tt[:], in_=ps, func=AF.Tanh, scale=0.5), psk, tt.k())
            elif fc < 4:
                tt = tgl[fc - 2]
                P.dve(lambda e: e.scalar_tensor_tensor(out=tt[:], in0=tt[:], scalar=1.0, in1=ps, op0=ALU.add, op1=ALU.mult),
                      tt.k() + psk, tt.k())
                P.act(lambda e: e.activation(out=zo[:, fc - 2, :], in_=tt[:], func=AF.Copy, scale=0.5), tt.k(), zo.k(fc - 2))
            else:
                copy_to(alt(), zo[:, fc - 2, :], ps, psk, zo.k(fc - 2))

        stream_proj("winab", l, list(range(7)), 8, lambda kc: hT[:, kc, :], lambda kc: hT.k(kc), T, evac)
        ret = hook() if hook is not None else None
        P.dma("pool", fm(zs[l], i), zo[:, 0:2, :], reads=zo.k(), writes=[("zs", l, i)])
        P.dma("pool", kTs[l][:, tile_cols(i)], zo[:, 2, :], reads=zo.k(), writes=[("kTs", l, i)])
        P.dma("pool", fm(ckTs[l], i), zo[:, 3:5, :], reads=zo.k(), writes=[("ckTs", l, i)])
        vo = P.get("vo")
        for sb in range(4):
            ps, psk = pj()
            for kc in range(8):
                P.pe(lambda e, kc=kc, sb=sb, ps=ps: e.matmul(ps[:, 0:384], lhsT=hT[:, kc, sb * 128:(sb + 1) * 128],
                                                             rhs=wva[:, kc * 384:(kc + 1) * 384], start=(kc == 0), stop=(kc == 7)),
                     hT.k(kc) + wva.k(), psk)
            copy_to(alt(), vo[:, sb, 0:128], ps[:, 0:128], psk, vo.k())
            pv_ = ps[:, 128:384].rearrange("p (j x) -> p j x", x=128)
            ov_ = vo[:, sb, 128:640].rearrange("p (j x) -> p j x", x=256)
            copy_to(alt(), ov_[:, :, 0:64], pv_[:, :, 0:64], psk, vo.k())
            copy_to(alt(), ov_[:, :, 192:256], pv_[:, :, 64:128], psk, vo.k())
        rows = slice(i * T, (i + 1) * T)
        P.dma("pool", vs[l][rows].rearrange("(s p) d -> p s d", p=128), vo[:, :, 0:128], reads=vo.k(), writes=[("vs", l, i)])
        P.dma("pool", cvs[l][rows].rearrange("(s p) d -> p s d", p=128), vo[:, :, 128:640], reads=vo.k(), writes=[("cvs", l, i)])
        return ret

    def conv_part1(l, i):
        zh = P.get("zh")
        lo = i * T - 15
        hi = (i + 1) * T + 15
        a0 = max(lo, 0)
        a1 = min(hi, S)
        rk = [("zs", l, j) for j in range(max(i - 1, 0), min(i + 1, NT - 1) + 1)]
        if lo < 0:
            P.dve(lambda e: e.memset(zh[:, :, 0:15], 0.0), [], zh.k())
        if hi > S:
            P.dve(lambda e: e.memset(zh[:, :, T + 15:T + 30], 0.0), [], zh.k())
        P.dma("sp", zh[:, :, a0 - lo:a1 - lo], zs[l].rearrange("(c p) t -> p c t", p=128)[:, :, a0:a1], reads=rk, writes=zh.k())
        cvf = P.get("cvf")
        cvb = P.get("cvb")
        for c in range(2):
            ps, psk = pj()
            for j in range(31):
                P.pe(lambda e, c=c, j=j, ps=ps: e.matmul(ps, lhsT=diag[:, c * 31 + j, :], rhs=zh[:, c, j:j + T], start=(j == 0), stop=(j == 30)),
                     ["diag"] + zh.k(), psk)
            P.act(lambda e, c=c, ps=ps: e.activation(out=cvf[:, c, :], in_=ps, func=AF.Identity, bias=vcol(("dwb", l), c), scale=1.0),
                  psk + ["vecs"], cvf.k(c))
            P.dve(lambda e, c=c: e.tensor_copy(out=cvb[:, c, :], in_=cvf[:, c, :]), cvf.k(c), cvb.k())
            P.act(lambda e, c=c: e.activation(out=cvb[:, 2 + c, :], in_=cvf[:, c, :], func=AF.Square), cvf.k(c), cvb.k())
        P.act(lambda e: e.activation(out=dummy[:, 3:4], in_=dummy[:, 2:3], func=AF.Ln), ["dummy"], ["dummy2"])
        pm, pmk = pj()
        for c in range(2):
            P.pe(lambda e, c=c: e.matmul(pm, lhsT=onesb[:], rhs=cvb[:, c, :], start=(c == 0), stop=(c == 1)), cvb.k() + ["onesb"], pmk)
        pq, pqk = pj()
        for c in range(2):
            P.pe(lambda e, c=c: e.matmul(pq, lhsT=onesb[:], rhs=cvb[:, 2 + c, :], start=(c == 0), stop=(c == 1)), cvb.k() + ["onesb"], pqk)
        mean = P.get("st")
        P.dve(lambda e: e.tensor_scalar(out=mean[:], in0=pm, scalar1=1.0 / 256, scalar2=None, op0=ALU.mult), pmk, mean.k())
        var = P.get("st")
        P.dve(lambda e: e.tensor_tensor(out=var[:], in0=mean[:], in1=mean[:], op=ALU.mult), mean.k(), var.k())
        P.dve(lambda e: e.scalar_tensor_tensor(out=var[:], in0=pq, scalar=1.0 / 256, in1=var[:], op0=ALU.mult, op1=ALU.subtract),
              pqk + var.k(), var.k())
        P.dve(lambda e: e.tensor_scalar(out=var[:], in0=var[:], scalar1=0.0, scalar2=None, op0=ALU.max), var.k(), var.k())
        P.act(lambda e: e.activation(out=var[:], in_=var[:], func=AF.Ln, bias=epsc[:, 0:1], scale=1.0), var.k() + ["epsc"], var.k())
        P.act(lambda e: e.activation(out=var[:], in_=var[:], func=AF.Exp, scale=-0.5), var.k(), var.k())
        yb = P.get("yb")
        for c in range(2):
            P.dve(lambda e, c=c: e.tensor_tensor(out=cvf[:, c, :], in0=cvf[:, c, :], in1=mean[:], op=ALU.subtract), cvf.k(c) + mean.k(), cvf.k(c))
            P.dve(lambda e, c=c: e.tensor_tensor(out=cvf[:, c, :], in0=cvf[:, c, :], in1=var[:], op=ALU.mult), cvf.k(c) + var.k(), cvf.k(c))
            P.dve(lambda e, c=c: e.tensor_scalar(out=cvf[:, c, :], in0=cvf[:, c, :], scalar1=vcol(("lng", l), c), scalar2=vcol(("lnb", l), c),
                                                 op0=ALU.mult, op1=ALU.add), cvf.k(c) + ["vecs"], cvf.k(c))
            tt = P.get("st")
            P.act(lambda e, c=c, tt=tt: e.activation(out=tt[:], in_=cvf[:, c, :], func=AF.Tanh, scale=0.5), cvf.k(c), tt.k())
            P.dve(lambda e, c=c, tt=tt: e.scalar_tensor_tensor(out=yb[:, c, :], in0=tt[:], scalar=1.0, in1=cvf[:, c, :], op0=ALU.add, op1=ALU.mult),
                  tt.k() + cvf.k(c), yb.k(c))
        return yb

    def conv_part2(l, acc, tg, yb):
        def evac(fc, ps, psk):
            tt = P.get("st")
            P.dve(lambda e: e.scalar_tensor_tensor(out=tt[:], in0=tg[:, fc, :], scalar=1.0, in1=ps, op0=ALU.add, op1=ALU.mult),
                  tg.k(fc) + psk, tt.k())
            P.act(lambda e: e.activation(out=acc[:, fc, :], in_=tt[:], func=AF.Copy, scale=0.5), tt.k(), acc.k(fc))

        stream_proj("cwo", l, list(range(8)), 2, lambda kc: yb[:, kc, :], lambda kc: yb.k(kc), T, evac)

    def gates_proj(l, hT, tg, base):
        def evac(fc, ps, psk):
            P.act(lambda e: e.activation(out=tg[:, fc - base, :], in_=ps, func=AF.Tanh, scale=0.5), psk, tg.k(fc - base))
        stream_proj("winb", l, list(range(base, base + 8)), 8, lambda kc: hT[:, kc, :], lambda kc: hT.k(kc), T, evac)

    def run_pipeline(iters, depth=1, lag=1):
        n = len(iters)
        if n == 0:
            return
        for j in range(min(depth, n)):
            iters[j]["s1"]()
        done = 0
        for k in range(n):
            if k + depth < n:
                iters[k + depth]["s1"]()
            if k >= lag:
                iters[k - lag]["s2b"]()
                done = k - lag + 1
            iters[k]["s2a"]()
        for j in range(done, n):
            iters[j]["s2b"]()

    PO = (psb[3][:, 0, :], [("ps", 6)])
    PD = (psb[3][:, 1, :], [("ps", 7)])

    def win_branch(l, i, hT, acc, tg):
        bq = P.get("bq")

        def evq(fc, ps, psk):
            copy_to(alt(), bq[:, fc, :], ps, psk, bq.k(fc), scale=0.125)
        stream_proj("winb", l, [0, 1, 2, 3], 8, lambda kc: hT[:, kc, :], lambda kc: hT.k(kc), T, evq)
        kh = P.get("kh")
        vlo = P.get("vlo")
        vhi = P.get("vhi")
        lo = i * T - 128
        hi = (i + 1) * T + 128
        a0, a1 = max(lo, 0), min(hi, S)
        tl = list(range(max(i - 1, 0), min(i + 1, NT - 1) + 1))
        if lo < 0:
            P.dve(lambda e: e.memset(kh[:, 0:128], 0.0), [], kh.k())
            P.dve(lambda e: e.memset(vlo[:, 0, :], 0.0), [], vlo.k())
            P.dve(lambda e: e.memset(vhi[:, 0, :], 0.0), [], vhi.k())
        if hi > S:
            P.dve(lambda e: e.memset(kh[:, 640:768], 0.0), [], kh.k())
            P.dve(lambda e: e.memset(vlo[:, 5, :], 0.0), [], vlo.k())
            P.dve(lambda e: e.memset(vhi[:, 5, :], 0.0), [], vhi.k())
        P.dma("sp", kh[:, a0 - lo:a1 - lo], kTs[l][:, a0:a1], reads=[("kTs", l, j) for j in tl], writes=kh.k())
        c0, c1 = (a0 - lo) // 128, (a1 - lo) // 128
        vsrc = vs[l][a0:a1].rearrange("(s p) d -> p s d", p=128)
        P.dma("sp", vlo[:, c0:c1, 0:64], vsrc[:, :, 0:64], reads=[("vs", l, j) for j in tl], writes=vlo.k())
        P.dma("sp", vhi[:, c0:c1, 64:128], vsrc[:, :, 64:128], reads=[("vs", l, j) for j in tl], writes=vhi.k())
        yb = P.get("yb")
        ppt = P.pools["pp"][0]
        ptt = P.pools["ptb"][0]
        iters = []
        kk = 0
        for qb in range(4):
            po, pok = (PO, PD)[qb % 2]
            for c in range(4):
                for hh in range(2):
                    st = {}
                    slot = kk % 4
                    kk += 1

                    def s1(qb=qb, c=c, hh=hh, st=st, slot=slot):
                        g = 4 * i + qb
                        edge = (g == 0) or (g == S // 128 - 1)
                        sc, sck = psb[slot // 2][:, slot % 2, :], [("ps", slot)]
                        P.pe(lambda e: e.matmul(sc[:, 0:384], lhsT=bq[hh * 64:(hh + 1) * 64, c, qb * 128:(qb + 1) * 128],
                                                rhs=kh[hh * 64:(hh + 1) * 64, qb * 128:qb * 128 + 384], start=True, stop=False),
                             bq.k(c) + kh.k(), sck)
                        P.pe(lambda e: e.matmul(sc[:, 0:384], lhsT=identb[:], rhs=t5tab[:, 2 * c + hh, :],
                                                start=False, stop=(not edge)), ["identb", "t5tab"], sck)
                        if edge:
                            v = 0 if g == 0 else 1
                            P.pe(lambda e, v=v: e.matmul(sc[:, 0:384], lhsT=onesb[0:1, :], rhs=cmask[0:1, v, :], start=False, stop=True),
                                 ["onesb", "cmask"], sck)
                        P.pe(lambda e: e.matmul(sc[:, 384:385], lhsT=identb[:], rhs=sinkb[:, 8 * l + 2 * c + hh:8 * l + 2 * c + hh + 1],
                                                start=True, stop=True), ["identb", "sinkb"], sck)
                        sm = P.get("sm")
                        ptile = ppt[(slot // 2) % 2]
                        pps = ptile[:, (slot % 2) * 512:(slot % 2) * 512 + 385]
                        ppk = ptile.k(slot % 2)
                        P.dve(lambda e: e.tensor_reduce(out=sm[:, 0:1], in_=sc[:, 0:385], axis=AX.X, op=ALU.max, negate=True), sck, sm.k())
                        P.act(lambda e: e.activation(out=pps, in_=sc[:, 0:385], func=AF.Exp, bias=sm[:, 0:1], scale=1.0, accum_out=sm[:, 8:9]),
                              sck + sm.k(), ppk + sm.k())
                        P.dve(lambda e: e.reciprocal(out=sm[:, 16:17], in_=sm[:, 8:9]), sm.k(), sm.k())
                        P.dve(lambda e: e.tensor_scalar(out=pps, in0=pps, scalar1=sm[:, 16:17], scalar2=None, op0=ALU.mult), ppk + sm.k(), ppk)
                        st["pps"] = pps
                        st["ppk"] = ppk

                    def s2a(st=st, slot=slot):
                        pps, ppk = st["pps"], st["ppk"]
                        pt, ptk = psb[2][:, slot % 2, :], [("ps", 4 + slot % 2)]
                        for s3 in range(3):
                            P.pe(lambda e, s3=s3: e.matmul(pt[:, s3 * 128:(s3 + 1) * 128], lhsT=pps[:, s3 * 128:(s3 + 1) * 128],
                                                           rhs=identb[:], start=True, stop=True), ppk + ["identb"], ptk)
                        ttile = ptt[(slot // 2) % 2]
                        pbs = ttile[:, (slot % 2) * 512:(slot % 2) * 512 + 384]
                        pbk = ttile.k(slot % 2)
                        copy_to("act", pbs, pt[:, 0:384], ptk, pbk)
                        st["pbs"] = pbs
                        st["pbk"] = pbk

                    def s2b(qb=qb, c=c, hh=hh, st=st, po=po, pok=pok):
                        pbs, pbk = st["pbs"], st["pbk"]
                        vv = vlo if hh == 0 else vhi
                        for s3 in range(3):
                            P.pe(lambda e, s3=s3: e.matmul(
                                po[:, c * 128:(c + 1) * 128], lhsT=vv[:, qb + s3, :], rhs=pbs[:, s3 * 128:(s3 + 1) * 128],
                                start=(hh == 0 and s3 == 0), stop=(hh == 1 and s3 == 2)), vv.k() + pbk, pok)
                        if c == 3 and hh == 1:
                            copy_to(alt(), yb[:, 0:4, qb * 128:(qb + 1) * 128], po.rearrange("p (c q) -> p c q", c=4), pok, yb.k())

                    iters.append({"s1": s1, "s2a": s2a, "s2b": s2b})
        run_pipeline(iters, depth=3, lag=2)
        gates_proj(l, hT, tg, 6 + 8)

        def evac(fc, ps, psk):
            tt = P.get("st")
            P.dve(lambda e: e.scalar_tensor_tensor(out=tt[:], in0=tg[:, fc, :], scalar=1.0, in1=ps, op0=ALU.add, op1=ALU.mult),
                  tg.k(fc) + psk, tt.k())
            P.dve(lambda e: e.tensor_tensor(out=acc[:, fc, :], in0=acc[:, fc, :], in1=tt[:], op=ALU.add), tt.k() + acc.k(fc), acc.k(fc))

        pj_state["set"] = [2, 3, 4, 5]
        stream_proj("wwo", l, list(range(8)), 4, lambda kc: yb[:, kc, :], lambda kc: yb.k(kc), T, evac)
        pj_state["set"] = list(range(8))

    def na_branch(l, i, hT, acc, tg, ybf):
        cq = P.get("cq")

        def evq(fc, ps, psk):
            copy_to(alt(), cq[:, fc - 4, :], ps, psk, cq.k(fc - 4), scale=0.125)
        stream_proj("winb", l, [4, 5], 8, lambda kc: hT[:, kc, :], lambda kc: hT.k(kc), T, evq)
        yb = P.get("yb")
        kvs = {}

        def load_kv(qb):
            j = 4 * i + qb
            kb = min(max(2 * j - 4, 0), 55)
            t0 = kb * 64
            tl = list(range(t0 // T, (t0 + 575) // T + 1))
            ckh = P.get("ckh")
            P.dma("sp", ckh[:], ckTs[l].rearrange("(c p) t -> p c t", p=128)[:, :, t0:t0 + 576],
                  reads=[("ckTs", l, x) for x in tl], writes=ckh.k())
            cvp = P.get("cvp")
            P.dma("sp", cvp[:, 0:4, :, :], cvs[l][t0:t0 + 512, :].rearrange("(s p) (h d) -> p s h d", p=128, h=4),
                  reads=[("cvs", l, x) for x in tl], writes=cvp.k())
            P.dma("sp", cvp[0:64, 4, :, :], cvs[l][t0 + 512:t0 + 576, :].rearrange("p (h d) -> p h d", h=4),
                  reads=[("cvs", l, x) for x in tl], writes=cvp.k())
            v = na_variant(j)
            if v == 2:
                nat, natk = natreg, ["natreg"]
            else:
                nt_ = P.get("nate")
                P.dma("sp", nt_[:], natb[l, v].rearrange("p (h k) -> p h k", h=4), reads=[("natb", l, v)], writes=nt_.k())
                nat, natk = nt_, nt_.k()
            kvs[qb] = dict(ckh=ckh, cvp=cvp, nat=nat, natk=natk)

        load_kv(0)
        iters = []
        for qb in range(4):
            po, pok = (PO, PD)[qb % 2]
            for pr in range(2):
                for hh in range(2):
                    st = {}

                    def s1(qb=qb, pr=pr, hh=hh, st=st):
                        qst = kvs[qb]
                        ckh, nat, natk = qst["ckh"], qst["nat"], qst["natk"]
                        h = 2 * pr + hh
                        sc, sck = sc2()
                        for part in range(2):
                            P.pe(lambda e, part=part: e.matmul(
                                sc[:, part, 0:288], lhsT=cq[hh * 64:(hh + 1) * 64, pr, qb * 128:(qb + 1) * 128],
                                rhs=ckh[hh * 64:(hh + 1) * 64, pr, part * 288:(part + 1) * 288], start=True, stop=False),
                                cq.k(pr) + ckh.k(), sck)
                        for part in range(2):
                            P.pe(lambda e, part=part: e.matmul(
                                sc[:, part, 0:288], lhsT=identb[:], rhs=nat[:, h, part * 288:(part + 1) * 288], start=False, stop=True),
                                ["identb"] + natk, sck)
                        pp = P.get("pp")
                        sm = P.get("sm")
                        P.dve(lambda e: e.tensor_reduce(out=sm[:, 0:1], in_=sc[:, :, 0:288], axis=AX.XY, op=ALU.max, negate=True), sck, sm.k())
                        P.act(lambda e: e.activation(out=pp[:, 0:576].rearrange("p (a b) -> p a b", a=2), in_=sc[:, :, 0:288], func=AF.Exp,
                                                     bias=sm[:, 0:1], scale=1.0, accum_out=sm[:, 8:9]), sck + sm.k(), pp.k() + sm.k())
                        P.dve(lambda e: e.reciprocal(out=sm[:, 16:17], in_=sm[:, 8:9]), sm.k(), sm.k())
                        P.dve(lambda e: e.tensor_scalar(out=pp[:, 0:576], in0=pp[:, 0:576], scalar1=sm[:, 16:17], scalar2=None, op0=ALU.mult),
                              pp.k() + sm.k(), pp.k())
                        st["pp"] = pp
                        if pr == 1 and hh == 0 and qb + 1 < 4:
                            load_kv(qb + 1)

                    def s2a(st=st):
                        pp = st["pp"]
                        pt, ptk = pt2()
                        for s5 in range(5):
                            m = 128 if s5 < 4 else 64
                            dst = pt[0:m, 0, s5 * 128:(s5 + 1) * 128] if s5 < 4 else pt[0:64, 1, 0:128]
                            P.pe(lambda e, s5=s5, m=m, dst=dst: e.matmul(dst, lhsT=pp[:, s5 * 128:s5 * 128 + m], rhs=identb[:], start=True, stop=True),
                                 pp.k() + ["identb"], ptk)
                        ptb = P.get("ptb")
                        copy_to(alt(), ptb[:, 0:512], pt[:, 0, :], ptk, ptb.k())
                        copy_to(alt(), ptb[0:64, 512:640], pt[0:64, 1, 0:128], ptk, ptb.k())
                        st["ptb"] = ptb

                    def s2b(qb=qb, pr=pr, hh=hh, st=st, po=po, pok=pok):
                        ptb = st["ptb"]
                        cvp = kvs[qb]["cvp"]
                        for s5 in range(5):
                            m = 128 if s5 < 4 else 64
                            n = hh * 5 + s5
                            P.pe(lambda e, s5=s5, m=m, n=n: e.matmul(
                                po[:, pr * 128:(pr + 1) * 128], lhsT=cvp[0:m, s5, 2 * pr + hh, :], rhs=ptb[0:m, s5 * 128:(s5 + 1) * 128],
                                start=(n == 0), stop=(n == 9)), cvp.k() + ptb.k(), pok)
                        if pr == 1 and hh == 1:
                            copy_to(alt(), yb[:, 0:2, qb * 128:(qb + 1) * 128], po[:, 0:256].rearrange("p (c q) -> p c q", c=2), pok, yb.k())

                    iters.append({"s1": s1, "s2a": s2a, "s2b": s2b})
        run_pipeline(iters)
        gates_proj(l, hT, tg, 6 + 16)

        def evac(fc, ps, psk):
            tt = P.get("st")
            P.dve(lambda e: e.scalar_tensor_tensor(out=tt[:], in0=tg[:, fc, :], scalar=1.0, in1=ps, op0=ALU.add, op1=ALU.mult),
                  tg.k(fc) + psk, tt.k())
            P.dve(lambda e: e.tensor_tensor(out=ybf[:, fc, :], in0=acc[:, fc, :], in1=tt[:], op=ALU.add), tt.k() + acc.k(fc), ybf.k(fc))

        pj_state["set"] = [2, 3, 4, 5]
        stream_proj("nwo", l, list(range(8)), 2, lambda kc: yb[:, kc, :], lambda kc: yb.k(kc), T, evac)
        pj_state["set"] = list(range(8))

    def cross_block(l, xt):
        hT = P.get("hT")
        rmsnorm(lambda c: xt[:, c, :], lambda c: xt.k(c), ("cross", l), lambda c: hT[:, c, :], lambda c: hT.k(c), T)
        xq_ = P.get("tg")
        xo_ = P.get("ybf")

        def evq(fc, ps, psk):
            copy_to(alt(), xq_[:, fc, :], ps, psk, xq_.k(fc), scale=0.0625)
        pj_state["set"] = [2, 3, 4, 5, 6, 7]
        stream_proj("xq", l, list(range(8)), 8, lambda kc: hT[:, kc, :], lambda kc: hT.k(kc), T, evq)
        po0, po0k = PO
        po1, po1k = PD
        iters = []
        for qb in range(4):
            st = {}

            def s1(qb=qb, st=st):
                qs = slice(qb * 128, (qb + 1) * 128)
                sc, sck = sc2()
                for h in range(4):
                    for kk in range(2):
                        P.pe(lambda e, h=h, kk=kk: e.matmul(sc[:, h // 2, (h % 2) * 256:(h % 2 + 1) * 256], lhsT=xq_[:, 2 * h + kk, qs],
                                                            rhs=KmT[:, 2 * h + kk, :], start=(kk == 0), stop=(kk == 1)),
                             xq_.k(2 * h + kk) + ["KmT"], sck)
                pp = P.get("pp")
                sm = P.get("sm")
                P.dve(lambda e: e.tensor_reduce(out=sm[:, 0:1], in_=sc[:, :, :], axis=AX.XY, op=ALU.max, negate=True), sck, sm.k())
                for h in range(4):
                    P.act(lambda e, h=h: e.activation(out=pp[:, h * 256:(h + 1) * 256], in_=sc[:, h // 2, (h % 2) * 256:(h % 2 + 1) * 256], func=AF.Exp,
                                                      bias=sm[:, 0:1], scale=1.0, accum_out=sm[:, 8 + h:9 + h]), sck + sm.k(), pp.k() + sm.k())
                P.dve(lambda e: e.reciprocal(out=sm[:, 16:20], in_=sm[:, 8:12]), sm.k(), sm.k())
                ppv = pp[:, 0:1024].rearrange("p (a b) -> p a b", a=4)
                P.dve(lambda e: e.tensor_tensor(out=ppv, in0=ppv, in1=sm[:, 16:20].unsqueeze(2).broadcast_to([128, 4, 256]), op=ALU.mult),
                      pp.k() + sm.k(), pp.k())
                st["pp"] = pp

            def s2a(st=st):
                pp = st["pp"]
                pt, ptk = pt2()
                for h in range(4):
                    for mc in range(2):
                        P.pe(lambda e, h=h, mc=mc: e.matmul(pt[:, h // 2, ((h % 2) * 2 + mc) * 128:((h % 2) * 2 + mc + 1) * 128],
                                                            lhsT=pp[:, h * 256 + mc * 128:h * 256 + (mc + 1) * 128], rhs=identb[:],
                                                            start=True, stop=True), pp.k() + ["identb"], ptk)
                ptb = P.get("ptb")
                copy_to(alt(), ptb[:, 0:1024].rearrange("p (a x) -> p a x", a=2), pt[:, :, :], ptk, ptb.k())
                st["ptb"] = ptb

            def s2b(qb=qb, st=st):
                qs = slice(qb * 128, (qb + 1) * 128)
                ptb = st["ptb"]
                for half, (po, pok) in enumerate(((po0, po0k), (po1, po1k))):
                    for hq in range(2):
                        h = 2 * half + hq
                        for dd in range(2):
                            for mc in range(2):
                                P.pe(lambda e, h=h, hq=hq, dd=dd, mc=mc, po=po: e.matmul(
                                    po[:, (hq * 2 + dd) * 128:(hq * 2 + dd + 1) * 128],
                                    lhsT=Vm[:, mc, h * 256 + dd * 128:h * 256 + (dd + 1) * 128],
                                    rhs=ptb[:, (h * 2 + mc) * 128:(h * 2 + mc + 1) * 128], start=(mc == 0), stop=(mc == 1)),
                                    ["Vm"] + ptb.k(), pok)
                    copy_to(alt(), xo_[:, 4 * half:4 * half + 4, qs], po.rearrange("p (c q) -> p c q", c=4), pok,
                            [k for c in range(4 * half, 4 * half + 4) for k in xo_.k(c)])

            iters.append({"s1": s1, "s2a": s2a, "s2b": s2b})
        run_pipeline(iters)
        pj_state["set"] = list(range(8))

        def evac(fc, ps, psk):
            P.dve(lambda e: e.tensor_tensor(out=xt[:, fc, :], in0=xt[:, fc, :], in1=ps, op=ALU.add), psk + xt.k(fc), xt.k(fc))
        stream_proj("xo", l, list(range(8)), 8, lambda kc: xo_[:, kc, :], lambda kc: xo_.k(kc), T, evac)

    def mem_prep(l):
        xm = P.get("acc")
        P.dma("sp", xm[:, :, 0:MEM], memT_in.rearrange("(c p) m -> p c m", p=128), writes=xm.k())
        hm = P.get("ybf")
        rmsnorm(lambda c: xm[:, c, 0:MEM], lambda c: xm.k(c), ("mem", l), lambda c: hm[:, c, 0:MEM], lambda c: hm.k(c), MEM)

        def evk(fc, ps, psk):
            copy_to(alt(), KmT[:, fc, :], ps[:, 0:MEM], psk, ["KmT"])
        stream_proj("xk", l, list(range(8)), 8, lambda kc: hm[:, kc, 0:MEM], lambda kc: hm.k(kc), MEM, evk)
        for half in range(2):
            wt = [P.get("w"), P.get("w")]
            for q in range(2):
                src = wb[("xv", l)].rearrange("p (k n) -> p k n", k=8)[:, 4 * q:4 * q + 4, half * 512:(half + 1) * 512]
                P.dma("sp", wt[q][:, 0:2048].rearrange("p (k n) -> p k n", k=4), src, reads=wkeys_flat("xv", l), writes=wt[q].k())
            for mc in range(2):
                ps, psk = pj()
                for kc in range(8):
                    w_ = wt[kc // 4]
                    P.pe(lambda e, kc=kc, mc=mc, ps=ps, w_=w_: e.matmul(ps, lhsT=hm[:, kc, mc * 128:(mc + 1) * 128],
                                                                       rhs=w_[:, (kc % 4) * 512:(kc % 4 + 1) * 512], start=(kc == 0), stop=(kc == 7)),
                         hm.k(kc) + w_.k(), psk)
                copy_to(alt(), Vm[:, mc, half * 512:(half + 1) * 512], ps, psk, ["Vm"])

    def layer_prep(l):
        for c in range(2):
            for j in range(31):
                P.dve(lambda e, c=c, j=j: e.tensor_scalar(out=diag[:, c * 31 + j, :], in0=identf[:], scalar1=vcol(("dw", l), c * 31 + j),
                                                          scalar2=None, op0=ALU.mult), ["identf", "vecs"], ["diag"])
        P.dma("sp", natreg[:], natb[l, 2].rearrange("p (h k) -> p h k", h=4), reads=[("natb", l, 2)], writes=["natreg"])
        mem_prep(l)

    def pass_b(l, i, xt):
        hT = P.get("hT")
        P.dma("sp", hT[:], fm(hTs[l], i), reads=[("hTs", l, i)], writes=hT.k())
        yb1 = conv_part1(l, i)
        acc = P.get("acc")
        tg = P.get("tg")
        ybf = P.get("ybf")
        gates_proj(l, hT, tg, 6)
        conv_part2(l, acc, tg, yb1)
        if stage >= 4:
            win_branch(l, i, hT, acc, tg)
        if stage >= 5:
            na_branch(l, i, hT, acc, tg, ybf)
        else:
            for fc in range(8):
                P.dve(lambda e, fc=fc: e.tensor_copy(out=ybf[:, fc, :], in_=acc[:, fc, :]), acc.k(fc), ybf.k(fc))

        def evac(fc, ps, psk):
            P.dve(lambda e: e.scalar_tensor_tensor(out=xt[:, fc, :], in0=ps, scalar=0.5, in1=xt[:, fc, :], op0=ALU.mult, op1=ALU.add),
                  psk + xt.k(fc), xt.k(fc))
        stream_proj("wo", l, list(range(8)), 8, lambda kc: ybf[:, kc, :], lambda kc: ybf.k(kc), T, evac)
        if stage >= 6:
            cross_block(l, xt)
        if stage >= 7:
            if l == 0:
                flush("late")
            ffn(l, 2, xt)

    def final_norm(i, xt):
        acc = P.get("acc")
        rmsnorm(lambda c: xt[:, c, :], lambda c: xt.k(c), "final", lambda c: acc[:, c, :], lambda c: acc.k(c), T)
        P.dma("pool", fm(outT, i), acc[:], reads=acc.k(), writes=[("out", i)])

    first = ["gu1", "wd1", "winab", "winaa"]
    rest = ["winb", "cwo", "wwo", "nwo", "wo", "xk", "xv", "xq", "xo", "gu2", "wd2"]
    for n in first:
        convert(n, 0)
    P.dma("pool", t5tab[:], t5_in.rearrange("p (h s) -> p h s", h=8), writes=["t5tab"])
    P.dma("pool", cmask[:], cmask_in.rearrange("p (v s) -> p v s", v=2), writes=["cmask"])

    def dump(i, xt):
        P.dma("sp", fm(outT, i), xt[:], reads=xt.k(), writes=[("out", i)])

    def load_x(src, i, rk):
        xt = P.get("xT")
        P.dma("sp", xt[:], fm(src, i), reads=rk, writes=xt.k())
        return xt

    early = ["xk", "xv", "winb", "cwo", "wwo", "nwo", "wo", "xq", "xo"]
    late = ["gu2", "wd2"]
    listA = []
    if stage >= 2:
        for l_ in range(L):
            for v_ in range(5):
                listA.append(("natb", lambda gate, l_=l_, v_=v_: P.dma("pool", natb[l_, v_], natab_in[l_, v_], reads=gate,
                                                                        writes=[("natb", l_, v_)], ring="conv")))
        for n in early:
            listA += [("early", it) for it in convert_items(n, 0)]
        for n in late:
            listA += [("late", it) for it in convert_items(n, 0)]
        if stage >= 8:
            for n in first:
                listA += [("first1", it) for it in convert_items(n, 1)]
            for n in early + late:
                listA += [("rest1", it) for it in convert_items(n, 1)]

    def emit_some(lst, k):
        gate = tick()
        for _ in range(min(k, len(lst))):
            lst.pop(0)(gate)

    CQ = {"lst": listA, "rate": 0.0, "acc": 0.0}

    def conv_point():
        CQ["acc"] += CQ["rate"]
        while CQ["acc"] >= 1.0 and CQ["lst"]:
            CQ["acc"] -= 1.0
            CQ["lst"].pop(0)[1](tick())
        if not CQ["lst"]:
            CQ["acc"] = 0.0

    def flush(tag):
        lst = CQ["lst"]
        while any(t == tag for t, _ in lst):
            lst.pop(0)[1](tick())

    nxt = load_x(xT_in, 0, [])
    for i in range(NT):
        xt = nxt
        if i == 1:
            HOOKS["conv_point"] = conv_point
            CQ["rate"] = 0.6
        hook = (lambda i=i: load_x(xT_in, i + 1, [])) if i + 1 < NT else None
        res_ = pass_a(0, i, xt, hook)
        if res_ is not None:
            nxt = res_
        if stage == 1:
            dump(i, xt)
        else:
            P.dma("pool", fm(xs, i), xt[:], reads=xt.k(), writes=[("xs", i)])
    for l in range(L):
        if stage < 2:
            break
        flush("early" if l == 0 else "rest1")
        layer_prep(l)
        nxt = load_x(xs, 0, [("xs", 0)])
        for i in range(NT):
            xt = nxt
            if l == 0 and i == 0:
                CQ["rate"] = 0.25
            pass_b(l, i, xt)
            if i + 1 < NT:
                nxt = load_x(xs, i + 1, [("xs", i + 1)])
            if stage < 8:
                dump(i, xt)
                continue
            if l + 1 < L:
                flush("first1")
                pass_a(l + 1, i, xt)
                P.dma("pool", fm(xs, i), xt[:], reads=xt.k(), writes=[("xs", i)])
            else:
                final_norm(i, xt)
        if stage < 8:
            break
    P.add("sp", lambda e: e.nop(), reads=[("out", i) for i in range(NT)])
    P.emit()
    P.close()
    return nc


_CACHE = {}


def kernel(stage=99, cores=8, **inputs):
    inp = {k: np.asarray(v) for k, v in inputs.items()}
    shared = prep_weights(inp)
    in_maps = []
    for b in range(cores):
        m = dict(shared)
        m["xT"] = np.ascontiguousarray(inp["x"][b].T)
        m["memT"] = np.ascontiguousarray(inp["mem"][b].T)
        in_maps.append(m)
    if stage not in _CACHE:
        _CACHE[stage] = build_program(stage)
    nc = _CACHE[stage]
    res = run_bass_kernel_spmd(nc, in_maps, core_ids=list(range(cores)))
    out = np.stack([np.ascontiguousarray(r["outT"].T) for r in res.results], axis=0)
    return out.astype(np.float32)
```

```python
import contextlib
import numpy as np
import concourse.bass as bass
import concourse.mybir as mybir
from concourse.bass_utils import run_bass_kernel_spmd

F32 = mybir.dt.float32
BF16 = mybir.dt.bfloat16
AF = mybir.ActivationFunctionType
ALU = mybir.AluOpType
AX = mybir.AxisListType

D = 1024
S = 4096
L = 2
T = 512
NT = S // T
DFF = 2816
NFF = DFF // 128
MEM = 256
EPS = 1e-6
NEG = -30000.0

EPOCH = 8192
RING = {"sp": 28, "pool": 16, "act": 8, "conv": 2, "conv0": 6}


class Op:
    __slots__ = ("eng", "fn", "reads", "writes", "dma", "deps", "sig", "sigval", "slot", "target", "idx", "ring")

    def __init__(self, eng, fn, reads, writes, dma):
        self.eng = eng
        self.fn = fn
        self.reads = reads
        self.writes = writes
        self.dma = dma
        self.deps = []
        self.sig = False
        self.sigval = 0
        self.slot = -1
        self.target = 0


class Tile:
    __slots__ = ("name", "buf", "nsub", "t")

    def __init__(self, t, name, buf, nsub):
        self.t = t
        self.name = name
        self.buf = buf
        self.nsub = nsub

    def k(self, i=None):
        if i is None:
            return [(self.name, self.buf, j) for j in range(self.nsub)]
        return [(self.name, self.buf, i)]

    def __getitem__(self, idx):
        return self.t[idx]


class View:
    def __init__(self, t, keyfn, n):
        self.t = t
        self.keyfn = keyfn
        self.n = n

    def k(self, i=None):
        if i is None:
            return [k for j in range(self.n) for k in self.keyfn(j)]
        return list(self.keyfn(i))

    def __getitem__(self, idx):
        return self.t[idx]


class Prog:
    def __init__(self, nc):
        self.nc = nc
        self.ops = []
        self.last_write = {}
        self.readers = {}
        self.stack = contextlib.ExitStack()
        self.ndma = {}
        self.dma_ops = {}
        self.pools = {}

    def sbuf(self, name, shape, dtype):
        return self.stack.enter_context(self.nc.sbuf_tensor("sb_" + name, list(shape), dtype))

    def psum(self, name, shape, dtype):
        return self.stack.enter_context(self.nc.psum_tensor("ps_" + name, list(shape), dtype))

    def pool(self, name, shape, dtype, bufs, nsub=1):
        tiles = []
        for b in range(bufs):
            t = self.sbuf(f"{name}_{b}", shape, dtype)
            tiles.append(Tile(t, name, b, nsub))
        self.pools[name] = [tiles, 0]
        return name

    def get(self, name):
        tiles, i = self.pools[name]
        self.pools[name][1] = i + 1
        return tiles[i % len(tiles)]

    def add(self, eng, fn, reads=(), writes=(), dma=False, ring=None):
        op = Op(eng, fn, list(reads), list(writes), dma)
        op.ring = ring or eng
        j = len(self.ops)
        op.idx = j
        deps = set()
        for k in op.reads:
            w = self.last_write.get(k)
            if w is not None:
                deps.add(w)
        for k in op.writes:
            w = self.last_write.get(k)
            if w is not None:
                deps.add(w)
            for r in self.readers.get(k, ()):
                deps.add(r)
        for k in op.writes:
            self.last_write[k] = j
            self.readers[k] = []
        ws = set(op.writes)
        for k in op.reads:
            if k not in ws:
                self.readers.setdefault(k, []).append(j)
        if dma:
            rn = op.ring
            n = self.ndma.get(rn, 0)
            ring = RING[rn]
            op.slot = n % ring
            op.target = 16 * (n // ring + 1)
            lst = self.dma_ops.setdefault(rn, [])
            if n >= ring:
                deps.add(lst[n - ring])
            lst.append(j)
            self.ndma[rn] = n + 1
            op.sig = True
        deps.discard(j)
        op.deps = sorted(deps)
        self.ops.append(op)
        return j

    def pe(self, fn, reads=(), writes=()):
        return self.add("pe", fn, reads, writes)

    def act(self, fn, reads=(), writes=()):
        return self.add("act", fn, reads, writes)

    def dve(self, fn, reads=(), writes=()):
        return self.add("dve", fn, reads, writes)

    def dma(self, eng, out, in_, reads=(), writes=(), ring=None):
        return self.add(eng, lambda e: e.dma_start(out=out, in_=in_), reads, writes, dma=True, ring=ring)

    def emit(self):
        nc = self.nc
        ops = self.ops

        def skip(p, op):
            return p.eng == "pe" and op.eng == "pe" and not op.dma and not p.dma

        for op in ops:
            for d in op.deps:
                p = ops[d]
                if p.dma or skip(p, op):
                    continue
                p.sig = True
        cnt = {}
        for op in ops:
            if op.dma or not op.sig:
                continue
            c = cnt.get(op.eng, 0)
            op.slot = c // EPOCH
            op.sigval = c % EPOCH + 1
            cnt[op.eng] = c + 1
        sems = {}
        for eng, c in cnt.items():
            for e in range((c + EPOCH - 1) // EPOCH):
                sems[(eng, e)] = self.stack.enter_context(nc.semaphore(f"s_{eng}_{e}"))
        for eng, n in self.ndma.items():
            for r in range(min(RING[eng], n)):
                sems[("dma" + eng, r)] = self.stack.enter_context(nc.semaphore(f"s_dma{eng}_{r}"))

        by_eng = {}
        for op in ops:
            by_eng.setdefault(op.eng, []).append(op)

        def run_engine(engname, e):
            known = {}
            for op in by_eng.get(engname, []):
                need = {}
                for d in op.deps:
                    p = ops[d]
                    if p.dma:
                        key = ("dma" + p.ring, p.slot)
                        val = p.target
                    else:
                        if skip(p, op):
                            continue
                        key = (p.eng, p.slot)
                        val = p.sigval
                    if known.get(key, 0) >= val:
                        continue
                    if need.get(key, 0) < val:
                        need[key] = val
                for key, val in need.items():
                    e.wait_ge(sems[key], val)
                    known[key] = val
                ins = op.fn(e)
                if op.dma:
                    ins.then_inc(sems[("dma" + op.ring, op.slot)], 16)
                elif op.sig:
                    ins.then_inc(sems[(op.eng, op.slot)], 1)

        with nc.Block() as block:
            @block.tensor
            def _(e):
                run_engine("pe", e)

            @block.scalar
            def _(e):
                run_engine("act", e)

            @block.vector
            def _(e):
                run_engine("dve", e)

            @block.gpsimd
            def _(e):
                run_engine("pool", e)

            @block.sync
            def _(e):
                run_engine("sp", e)

    def close(self):
        self.stack.close()


def opt_b(w):
    K, F = w.shape
    kc, fc = K // 128, F // 128
    return np.ascontiguousarray(w.reshape(kc, 128, fc, 128).transpose(2, 1, 0, 3).reshape(fc, 128, kc * 128))


def opt_a(w):
    K, N = w.shape
    kc = K // 128
    return np.ascontiguousarray(w.reshape(kc, 128, N).transpose(1, 0, 2).reshape(128, kc * N))


def cols(v):
    return np.ascontiguousarray(v.reshape(-1, 128).T)


def t5_buckets(rel):
    half = 16
    max_exact = 8
    ret = (rel > 0).astype(np.int32) * half
    n = np.abs(rel)
    large = max_exact + (np.log(np.maximum(n, 1) / max_exact) / np.log(128 / max_exact) * (half - max_exact)).astype(np.int32)
    large = np.minimum(large, half - 1)
    return ret + np.where(n < max_exact, n, large)


def na_geometry():
    reps = [0, 1, 2, 30, 31]
    drow = np.zeros((5, 128, 576), np.int64)
    dcol = np.zeros((5, 128, 576), np.int64)
    valid = np.zeros((5, 128, 576), bool)
    for v, j in enumerate(reps):
        kb = min(max(2 * j - 4, 0), 55)
        for q in range(128):
            r = 2 * j + q // 64
            c = q % 64
            rs = min(max(r - 4, 0), 56)
            cs = min(max(c - 8, 0), 48)
            kr = kb + np.arange(576) // 64
            kc = np.arange(576) % 64
            ok = (kr >= rs) & (kr < rs + 8) & (kc >= cs) & (kc < cs + 16)
            valid[v, q] = ok
            drow[v, q] = np.where(ok, kr - r + 7, 0)
            dcol[v, q] = np.where(ok, kc - c + 15, 0)
    return drow, dcol, valid


def na_variant(j):
    if j <= 1:
        return j
    if j >= 30:
        return j - 27
    return 2


WNAMES = ["gu1", "wd1", "winab", "winaa", "winb", "cwo", "wwo", "nwo", "wo", "xq", "xk", "xv", "xo", "gu2", "wd2"]
QPERM = [0, 4, 1, 5, 2, 6, 3, 7]


def prep_weights(inp):
    out = {}
    for l in range(L):
        for n, tag in ((1, "ffn1"), (2, "ffn2")):
            g = opt_b(inp[f"{tag}_w_gate"][l])
            u = opt_b(inp[f"{tag}_w_up"][l])
            out[f"gu{n}_{l}"] = np.ascontiguousarray(np.concatenate([g, u], axis=2))
            out[f"wd{n}_{l}"] = opt_b(inp[f"{tag}_w_down"][l])
        w = inp["w_in"][l]
        ua, bq, bk, bv, cq, ck, cv, gt = np.split(w, [512, 1024, 1152, 1280, 1536, 1792, 2048], axis=1)
        a, gate = ua[:, :256], ua[:, 256:]
        out[f"winab_{l}"] = opt_b(np.concatenate([gate, a, bk, ck], axis=1))
        out[f"winaa_{l}"] = opt_a(np.concatenate([bv, cv], axis=1))
        bqp = bq.reshape(D, 8, 64)[:, QPERM, :].reshape(D, 512)
        out[f"winb_{l}"] = opt_b(np.concatenate([bqp, cq, gt], axis=1))
        out[f"cwo_{l}"] = opt_b(inp["conv_w_out"][l])
        wwo = inp["win_w_out"][l].reshape(8, 64, D)[QPERM].reshape(512, D)
        out[f"wwo_{l}"] = opt_b(wwo)
        out[f"nwo_{l}"] = opt_b(inp["na_w_out"][l])
        out[f"wo_{l}"] = opt_b(inp["w_out"][l])
        out[f"xq_{l}"] = opt_b(inp["cross_w_q"][l])
        out[f"xk_{l}"] = opt_b(inp["cross_w_kv"][l][:, :D])
        out[f"xv_{l}"] = opt_a(inp["cross_w_kv"][l][:, D:])
        out[f"xo_{l}"] = opt_b(inp["cross_w_o"][l])
    vec = []
    for l in range(L):
        for nm in ("norm_ffn1", "norm_mix", "norm_cross", "norm_mem", "norm_ffn2"):
            vec.append(cols(inp[nm][l]))
    vec.append(cols(inp["norm_final"]))
    for l in range(L):
        dw = inp["conv_dw_w"][l]
        vec.append(np.ascontiguousarray(dw.T.reshape(2, 128, 31).transpose(1, 0, 2).reshape(128, 62)))
    for l in range(L):
        vec.append(cols(inp["conv_dw_b"][l]))
        vec.append(cols(inp["conv_ln_g"][l]))
        vec.append(cols(inp["conv_ln_b"][l]))
    for l in range(L):
        sk = inp["win_sink"][l][QPERM]
        vec.append(np.ascontiguousarray(np.broadcast_to(sk[None, :], (128, 8))))
    out["vecs"] = np.ascontiguousarray(np.concatenate(vec, axis=1).astype(np.float32))
    out["ident"] = np.eye(128, dtype=np.float32)
    rel = np.arange(384)[None, :] - 128 - np.arange(128)[:, None]
    band = np.abs(rel) <= 128
    tb = inp["t5_bias"][t5_buckets(rel)]
    tb = np.where(band[:, :, None], tb, np.float32(NEG)).transpose(0, 2, 1)[:, QPERM, :]
    out["t5tab"] = np.ascontiguousarray(tb.reshape(128, 8 * 384).astype(np.float32))
    cm = np.zeros((1, 2, 384), np.float32)
    cm[0, 0, :128] = NEG
    cm[0, 1, 256:] = NEG
    out["colmask"] = cm.reshape(1, 768)
    drow, dcol, valid = na_geometry()
    nat = np.zeros((L, 5, 128, 4, 576), np.float32)
    for l in range(L):
        for h in range(4):
            g = inp["na_rpb"][l][h][drow, dcol]
            nat[l, :, :, h, :] = np.where(valid, g, np.float32(NEG))
    out["natab"] = np.ascontiguousarray(nat.reshape(L, 5, 128, 4 * 576))
    return out


VOFF = {}


def _vec_offsets():
    o = 0
    for l in range(L):
        for nm in ("ffn1", "mix", "cross", "mem", "ffn2"):
            VOFF[(nm, l)] = o
            o += 8
    VOFF["final"] = o
    o += 8
    for l in range(L):
        VOFF[("dw", l)] = o
        o += 62
    for l in range(L):
        VOFF[("dwb", l)] = o
        o += 2
        VOFF[("lng", l)] = o
        o += 2
        VOFF[("lnb", l)] = o
        o += 2
    for l in range(L):
        VOFF[("sink", l)] = o
        o += 8
    return o


NVEC = _vec_offsets()

WSHAPES = {
    "gu1": [NFF, 128, 2048], "gu2": [NFF, 128, 2048], "wd1": [8, 128, DFF], "wd2": [8, 128, DFF],
    "winab": [7, 128, 1024], "winaa": [128, 8 * 384], "winb": [30, 128, 1024],
    "cwo": [8, 128, 256], "wwo": [8, 128, 512], "nwo": [8, 128, 256], "wo": [8, 128, 1024],
    "xq": [8, 128, 1024], "xk": [8, 128, 1024], "xv": [128, 8 * 1024], "xo": [8, 128, 1024],
}
WGROUP = {"gu1": 2, "gu2": 2, "wd1": 1, "wd2": 1, "winab": 4, "winb": 4, "cwo": 8, "wwo": 8, "nwo": 8,
          "wo": 4, "xq": 4, "xk": 4, "xo": 4}


def build_program(stage=99, debug_out=False):
    nc = bass.Bass("TRN2", target_bir_lowering=False)
    P = Prog(nc)
    dt = {}

    def din(name, shape):
        dt[name] = nc.dram_tensor(name, list(shape), F32, kind="ExternalInput").ap()
        return dt[name]

    xT_in = din("xT", [D, S])
    memT_in = din("memT", [D, MEM])
    vecs_in = din("vecs", [128, NVEC])
    ident_in = din("ident", [128, 128])
    t5_in = din("t5tab", [128, 8 * 384])
    cmask_in = din("colmask", [1, 768])
    natab_in = din("natab", [L, 5, 128, 4 * 576])
    wf = {}
    wb = {}
    for l in range(L):
        for n in WNAMES:
            wf[(n, l)] = din(f"{n}_{l}", WSHAPES[n])
            wb[(n, l)] = nc.dram_tensor(f"b_{n}_{l}", WSHAPES[n], BF16).ap()
    natb = nc.dram_tensor("natb", [L, 5, 128, 4 * 576], BF16).ap()
    outT = nc.dram_tensor("outT", [D, S], F32, kind="ExternalOutput").ap()
    xs = nc.dram_tensor("xs", [D, S], F32).ap()
    hTs = [nc.dram_tensor(f"hTs{l}", [D, S], BF16).ap() for l in range(L)]
    zs = [nc.dram_tensor(f"zs{l}", [256, S], BF16).ap() for l in range(L)]
    kTs = [nc.dram_tensor(f"kTs{l}", [128, S], BF16).ap() for l in range(L)]
    vs = [nc.dram_tensor(f"vs{l}", [S, 128], BF16).ap() for l in range(L)]
    ckTs = [nc.dram_tensor(f"ckTs{l}", [256, S], BF16).ap() for l in range(L)]
    cvs = [nc.dram_tensor(f"cvs{l}", [S, 512], BF16).ap() for l in range(L)]

    vecs = P.sbuf("vecs", [128, NVEC], F32)
    identf = P.sbuf("identf", [128, 128], F32)
    identb = P.sbuf("identb", [128, 128], BF16)
    onesb = P.sbuf("onesb", [128, 128], BF16)
    t5tab = P.sbuf("t5tab", [128, 8, 384], BF16)
    cmask = P.sbuf("cmask", [1, 2, 384], BF16)
    natreg = P.sbuf("natreg", [128, 4, 576], BF16)
    KmT = P.sbuf("KmT", [128, 8, MEM], BF16)
    Vm = P.sbuf("Vm", [128, 2, D], BF16)
    diag = P.sbuf("diag", [128, 62, 128], BF16)
    P.pool("xT", [128, 8, T], F32, 2, nsub=8)
    P.pool("hT", [128, 8, T], BF16, 1, nsub=8)
    P.pool("w", [128, 4096], BF16, 3)
    P.pool("sq", [128, T], BF16, 3)
    P.pool("st", [128, T], F32, 4)
    P.pool("tg", [128, 8, T], BF16, 1, nsub=8)
    P.pool("bq", [128, 4, T], BF16, 1, nsub=4)
    P.pool("cq", [128, 2, T], BF16, 1, nsub=2)
    P.pool("kh", [128, 768], BF16, 1)
    P.pool("vlo", [128, 6, 128], BF16, 1)
    P.pool("vhi", [128, 6, 128], BF16, 1)
    P.pool("ckh", [128, 2, 576], BF16, 2)
    P.pool("cvp", [128, 5, 4, 128], BF16, 2)
    P.pool("nate", [128, 4, 576], BF16, 2)
    P.pool("pp", [128, 1024], BF16, 2, nsub=2)
    P.pool("ptb", [128, 1024], BF16, 2, nsub=2)
    P.pool("sm", [128, 32], F32, 6)
    P.pool("yb", [128, 5, T], BF16, 1, nsub=5)
    P.pool("zh", [128, 2, T + 30], BF16, 1)
    P.pool("cvf", [128, 2, T], F32, 1, nsub=2)
    P.pool("cvb", [128, 4, T], BF16, 1)
    P.pool("vo", [128, 4, 128 + 512], BF16, 1)
    big = P.sbuf("big", [128, 24 * T], BF16)
    actT_v = View(big[:, 0:NFF * T].rearrange("p (c t) -> p c t", t=T), lambda c: [("big", c)], NFF)
    acc_v = View(big[:, 0:16 * T].bitcast(F32).rearrange("p (c t) -> p c t", t=T), lambda c: [("big", 2 * c), ("big", 2 * c + 1)], 8)
    ybf_v = View(big[:, 16 * T:24 * T].rearrange("p (c t) -> p c t", t=T), lambda c: [("big", 16 + c)], 8)
    views = {"actT": actT_v, "acc": acc_v, "ybf": ybf_v}
    _get = P.get

    def get(name):
        if name in views:
            return views[name]
        return _get(name)
    P.get = get
    psb = [P.psum(f"psb{i}", [128, 2, 512], F32) for i in range(4)]

    pj_state = {"i": 0, "set": list(range(8))}

    def pj():
        s = pj_state["set"]
        h = s[pj_state["i"] % len(s)]
        pj_state["i"] += 1
        return psb[h // 2][:, h % 2, :], [("ps", h)]

    sc_state = {"i": 0}

    def sc2():
        b = sc_state["i"] % 2
        sc_state["i"] += 1
        return psb[b], [("ps", 2 * b), ("ps", 2 * b + 1)]

    def pt2():
        return psb[2], [("ps", 4), ("ps", 5)]

    def vcol(key, c0=0, n=1):
        o = VOFF[key] + c0
        return vecs[:, o:o + n]

    ev = {"i": 0}

    def alt():
        ev["i"] += 1
        return "act" if ev["i"] % 2 else "dve"

    def copy_to(eng, out, in_, reads, writes, scale=None):
        if eng == "act":
            if scale is None:
                P.act(lambda e: e.activation(out=out, in_=in_, func=AF.Copy), reads, writes)
            else:
                P.act(lambda e: e.activation(out=out, in_=in_, func=AF.Copy, scale=scale), reads, writes)
        else:
            if scale is None:
                P.dve(lambda e: e.tensor_copy(out=out, in_=in_), reads, writes)
            else:
                P.dve(lambda e: e.tensor_scalar(out=out, in0=in_, scalar1=scale, scalar2=None, op0=ALU.mult), reads, writes)

    epsc = P.sbuf("epsc", [128, 1], F32)
    P.dve(lambda e: e.memset(epsc[:], EPS), [], ["epsc"])
    dummy = P.sbuf("dummy", [128, 4], F32)
    P.dve(lambda e: e.memset(dummy[:, 2:3], 1.0), [], ["dummy"])

    P.dma("sp", vecs[:], vecs_in, writes=["vecs"])
    sinkb = P.sbuf("sinkb", [128, 8 * L], BF16)
    P.dve(lambda e: e.tensor_copy(out=sinkb[:], in_=vecs[:, VOFF[("sink", 0)]:VOFF[("sink", 0)] + 8 * L]), ["vecs"], ["sinkb"])
    P.dma("sp", identf[:], ident_in, writes=["identf"])
    P.dve(lambda e: e.tensor_copy(out=identb[:], in_=identf[:]), ["identf"], ["identb"])
    P.dve(lambda e: e.memset(onesb[:], 1.0), [], ["onesb"])

    for nm in ("vlo", "vhi", "vo"):
        for tl_ in P.pools[nm][0]:
            P.dve(lambda e, tl_=tl_: e.memset(tl_[:], 0.0), [], tl_.k())

    def convert_items(n, l):
        src, dst = wf[(n, l)], wb[(n, l)]
        items = []
        if n in ("winaa", "xv"):
            ncol = WSHAPES[n][1]
            step = 2048
            for c0 in range(0, ncol, step):
                c1 = min(ncol, c0 + step)
                wk = [("wb", n, l, "all")] if c0 == 0 else [("wbx", n, l, c0)]
                items.append(lambda gate, c0=c0, c1=c1, wk=wk, ring="conv": P.dma("pool", dst[:, c0:c1], src[:, c0:c1], reads=gate, writes=wk, ring=ring))
            return items
        g = WGROUP[n]
        nfc = WSHAPES[n][0]
        for f0 in range(0, nfc, g):
            f1 = min(nfc, f0 + g)
            items.append(lambda gate, f0=f0, f1=f1, ring="conv": P.dma("pool", dst[f0:f1].rearrange("g p k -> p g k"), src[f0:f1].rearrange("g p k -> p g k"),
                                                                       reads=gate, writes=[("wb", n, l, f0 // g)], ring=ring))
        return items

    def convert(n, l):
        for it in convert_items(n, l):
            it([], ring="conv0")

    tick_state = {"n": 0}

    def tick():
        k = ("tick", tick_state["n"])
        tick_state["n"] += 1
        P.dve(lambda e: e.memset(dummy[:, 0:1], 1.0), [], [k, "tickmem"])
        return [k]

    def wkeys_flat(n, l):
        ncol = WSHAPES[n][1]
        return [("wb", n, l, "all")] + [("wbx", n, l, c0) for c0 in range(2048, ncol, 2048)]

    def stream_proj(n, l, fcs, KC, rhs_fn, rhs_keys, N, evac):
        g = WGROUP[n]
        per = KC * 128
        groups = {}
        for fc in fcs:
            groups.setdefault(fc // g, []).append(fc)
        for gi, lst in groups.items():
            wt = P.get("w")
            f0 = gi * g
            f1 = min(WSHAPES[n][0], f0 + g)
            nf = f1 - f0
            wsl = wb[(n, l)][f0:f1].rearrange("g p k -> p g k")
            P.dma("sp", wt[:, 0:nf * per].rearrange("p (g k) -> p g k", g=nf), wsl,
                  reads=[("wb", n, l, gi)], writes=wt.k())
            for fc in lst:
                base = (fc - f0) * per
                ps, psk = pj()
                for kc in range(KC):
                    P.pe(lambda e, ps=ps, wt=wt, o=base + kc * 128, kc=kc: e.matmul(
                        ps[:, 0:N], lhsT=wt[:, o:o + 128], rhs=rhs_fn(kc), start=(kc == 0), stop=(kc == KC - 1)),
                        reads=wt.k() + rhs_keys(kc), writes=psk)
                evac(fc, ps, psk)

    STATE = {"pool_ok": False}
    HOOKS = {"conv_point": lambda: None}

    def rstd_from(ps, psk, n_feat, N):
        ms = P.get("st")
        P.act(lambda e: e.activation(out=ms[:, 0:N], in_=ps[:, 0:N], func=AF.Ln, bias=epsc[:, 0:1], scale=1.0 / n_feat), psk + ["epsc"], ms.k())
        P.act(lambda e: e.activation(out=ms[:, 0:N], in_=ms[:, 0:N], func=AF.Exp, scale=-0.5), ms.k(), ms.k())
        return ms

    def rmsnorm(src, src_keys, gkey, dst, dst_keys, N, out_f32=None):
        ps, psk = pj()
        for c in range(8):
            sq = P.get("sq")
            P.act(lambda e, c=c, sq=sq: e.activation(out=sq[:, 0:N], in_=src(c), func=AF.Square), src_keys(c), sq.k())
            P.pe(lambda e, c=c, sq=sq: e.matmul(ps[:, 0:N], lhsT=onesb[:], rhs=sq[:, 0:N], start=(c == 0), stop=(c == 7)),
                 sq.k() + ["onesb"], psk)
        P.act(lambda e: e.activation(out=dummy[:, 3:4], in_=dummy[:, 2:3], func=AF.Ln), ["dummy"], ["dummy2"])
        rs = rstd_from(ps, psk, D, N)
        for c in range(8):
            if STATE["pool_ok"] and c % 2 == 1:
                eng = lambda fn, r, w: P.add("pool", fn, r, w)
            else:
                eng = P.dve
            eng(lambda e, c=c: e.scalar_tensor_tensor(out=dst(c), in0=src(c), scalar=vcol(gkey, c), in1=rs[:, 0:N],
                                                      op0=ALU.mult, op1=ALU.mult),
                src_keys(c) + rs.k() + ["vecs"], dst_keys(c))

    def ffn(l, which, xt):
        hT = P.get("hT")
        rmsnorm(lambda c: xt[:, c, :], lambda c: xt.k(c), ("ffn%d" % which, l), lambda c: hT[:, c, :], lambda c: hT.k(c), T)
        aT = P.get("actT")
        gun = "gu%d" % which
        g = WGROUP[gun]
        for f0 in range(0, NFF, g):
            wt = P.get("w")
            wsl = wb[(gun, l)][f0:f0 + g].rearrange("g p k -> p g k")
            P.dma("sp", wt[:, 0:g * 2048].rearrange("p (g k) -> p g k", g=g), wsl, reads=[("wb", gun, l, f0 // g)], writes=wt.k())
            for fi in range(g):
                fc = f0 + fi
                psg, pgk = pj()
                psu, puk = pj()
                for kc in range(8):
                    P.pe(lambda e, o=fi * 2048 + kc * 128, kc=kc, wt=wt, psg=psg: e.matmul(
                        psg, lhsT=wt[:, o:o + 128], rhs=hT[:, kc, :], start=(kc == 0), stop=(kc == 7)),
                        wt.k() + hT.k(kc), pgk)
                for kc in range(8):
                    P.pe(lambda e, o=fi * 2048 + 1024 + kc * 128, kc=kc, wt=wt, psu=psu: e.matmul(
                        psu, lhsT=wt[:, o:o + 128], rhs=hT[:, kc, :], start=(kc == 0), stop=(kc == 7)),
                        wt.k() + hT.k(kc), puk)
                tt = P.get("st")
                P.act(lambda e, tt=tt, psg=psg: e.activation(out=tt[:], in_=psg, func=AF.Tanh, scale=0.5), pgk, tt.k())
                P.dve(lambda e, tt=tt, psg=psg: e.scalar_tensor_tensor(out=tt[:], in0=tt[:], scalar=1.0, in1=psg,
                                                                         op0=ALU.add, op1=ALU.mult), tt.k() + pgk, tt.k())
                P.dve(lambda e, tt=tt, psu=psu, fc=fc: e.tensor_tensor(out=aT[:, fc, :], in0=tt[:], in1=psu, op=ALU.mult),
                      tt.k() + puk, aT.k(fc))
            HOOKS["conv_point"]()
        wdn = "wd%d" % which

        def evac(dc, ps, psk):
            P.dve(lambda e: e.scalar_tensor_tensor(out=xt[:, dc, :], in0=ps, scalar=0.25, in1=xt[:, dc, :],
                                                   op0=ALU.mult, op1=ALU.add), psk + xt.k(dc), xt.k(dc))
            HOOKS["conv_point"]()

        stream_proj(wdn, l, list(range(8)), NFF, lambda kc: aT[:, kc, :], lambda kc: aT.k(kc), T, evac)

    def tile_cols(i):
        return slice(i * T, (i + 1) * T)

    def fm(ap, i):
        return ap.rearrange("(c p) t -> p c t", p=128)[:, :, tile_cols(i)]

    def pass_a(l, i, xt, hook=None):
        ffn(l, 1, xt)
        wva = P.get("w")
        P.dma("sp", wva[:, 0:8 * 384], wb[("winaa", l)], reads=wkeys_flat("winaa", l), writes=wva.k())
        hT = P.get("hT")
        rmsnorm(lambda c: xt[:, c, :], lambda c: xt.k(c), ("mix", l), lambda c: hT[:, c, :], lambda c: hT.k(c), T)
        P.dma("pool", fm(hTs[l], i), hT[:], reads=hT.k(), writes=[("hTs", l, i)])
        zo = P.get("yb")
        tgl = [None, None]

        def evac(fc, ps, psk):
            if fc < 2:
                tt = P.get("st")
                tgl[fc] = tt
                P.act(lambda e: e.activation(out=tt[:], in_=ps, func=AF.Tanh, scale=0.5), psk, tt.k())
            elif fc < 4:
                tt = tgl[fc - 2]
                P.dve(lambda e: e.scalar_tensor_tensor(out=tt[:], in0=tt[:], scalar=1.0, in1=ps, op0=ALU.add, op1=ALU.mult),
                      tt.k() + psk, tt.k())
                P.act(lambda e: e.activation(out=zo[:, fc - 2, :], in_=tt[:], func=AF.Copy, scale=0.5), tt.k(), zo.k(fc - 2))
            else:
                copy_to(alt(), zo[:, fc - 2, :], ps, psk, zo.k(fc - 2))

        stream_proj("winab", l, list(range(7)), 8, lambda kc: hT[:, kc, :], lambda kc: hT.k(kc), T, evac)
        ret = hook() if hook is not None else None
        P.dma("pool", fm(zs[l], i), zo[:, 0:2, :], reads=zo.k(), writes=[("zs", l, i)])
        P.dma("pool", kTs[l][:, tile_cols(i)], zo[:, 2, :], reads=zo.k(), writes=[("kTs", l, i)])
        P.dma("pool", fm(ckTs[l], i), zo[:, 3:5, :], reads=zo.k(), writes=[("ckTs", l, i)])
        vo = P.get("vo")
        for sb in range(4):
            ps, psk = pj()
            for kc in range(8):
                P.pe(lambda e, kc=kc, sb=sb, ps=ps: e.matmul(ps[:, 0:384], lhsT=hT[:, kc, sb * 128:(sb + 1) * 128],
                                                             rhs=wva[:, kc * 384:(kc + 1) * 384], start=(kc == 0), stop=(kc == 7)),
                     hT.k(kc) + wva.k(), psk)
            copy_to(alt(), vo[:, sb, 0:128], ps[:, 0:128], psk, vo.k())
            pv_ = ps[:, 128:384].rearrange("p (j x) -> p j x", x=128)
            ov_ = vo[:, sb, 128:640].rearrange("p (j x) -> p j x", x=256)
            copy_to(alt(), ov_[:, :, 0:64], pv_[:, :, 0:64], psk, vo.k())
            copy_to(alt(), ov_[:, :, 192:256], pv_[:, :, 64:128], psk, vo.k())
        rows = slice(i * T, (i + 1) * T)
        P.dma("pool", vs[l][rows].rearrange("(s p) d -> p s d", p=128), vo[:, :, 0:128], reads=vo.k(), writes=[("vs", l, i)])
        P.dma("pool", cvs[l][rows].rearrange("(s p) d -> p s d", p=128), vo[:, :, 128:640], reads=vo.k(), writes=[("cvs", l, i)])
        return ret

    def conv_part1(l, i):
        zh = P.get("zh")
        lo = i * T - 15
        hi = (i + 1) * T + 15
        a0 = max(lo, 0)
        a1 = min(hi, S)
        rk = [("zs", l, j) for j in range(max(i - 1, 0), min(i + 1, NT - 1) + 1)]
        if lo < 0:
            P.dve(lambda e: e.memset(zh[:, :, 0:15], 0.0), [], zh.k())
        if hi > S:
            P.dve(lambda e: e.memset(zh[:, :, T + 15:T + 30], 0.0), [], zh.k())
        P.dma("sp", zh[:, :, a0 - lo:a1 - lo], zs[l].rearrange("(c p) t -> p c t", p=128)[:, :, a0:a1], reads=rk, writes=zh.k())
        cvf = P.get("cvf")
        cvb = P.get("cvb")
        for c in range(2):
            ps, psk = pj()
            for j in range(31):
                P.pe(lambda e, c=c, j=j, ps=ps: e.matmul(ps, lhsT=diag[:, c * 31 + j, :], rhs=zh[:, c, j:j + T], start=(j == 0), stop=(j == 30)),
                     ["diag"] + zh.k(), psk)
            P.act(lambda e, c=c, ps=ps: e.activation(out=cvf[:, c, :], in_=ps, func=AF.Identity, bias=vcol(("dwb", l), c), scale=1.0),
                  psk + ["vecs"], cvf.k(c))
            P.dve(lambda e, c=c: e.tensor_copy(out=cvb[:, c, :], in_=cvf[:, c, :]), cvf.k(c), cvb.k())
            P.act(lambda e, c=c: e.activation(out=cvb[:, 2 + c, :], in_=cvf[:, c, :], func=AF.Square), cvf.k(c), cvb.k())
        P.act(lambda e: e.activation(out=dummy[:, 3:4], in_=dummy[:, 2:3], func=AF.Ln), ["dummy"], ["dummy2"])
        pm, pmk = pj()
        for c in range(2):
            P.pe(lambda e, c=c: e.matmul(pm, lhsT=onesb[:], rhs=cvb[:, c, :], start=(c == 0), stop=(c == 1)), cvb.k() + ["onesb"], pmk)
        pq, pqk = pj()
        for c in range(2):
            P.pe(lambda e, c=c: e.matmul(pq, lhsT=onesb[:], rhs=cvb[:, 2 + c, :], start=(c == 0), stop=(c == 1)), cvb.k() + ["onesb"], pqk)
        mean = P.get("st")
        P.dve(lambda e: e.tensor_scalar(out=mean[:], in0=pm, scalar1=1.0 / 256, scalar2=None, op0=ALU.mult), pmk, mean.k())
        var = P.get("st")
        P.dve(lambda e: e.tensor_tensor(out=var[:], in0=mean[:], in1=mean[:], op=ALU.mult), mean.k(), var.k())
        P.dve(lambda e: e.scalar_tensor_tensor(out=var[:], in0=pq, scalar=1.0 / 256, in1=var[:], op0=ALU.mult, op1=ALU.subtract),
              pqk + var.k(), var.k())
        P.dve(lambda e: e.tensor_scalar(out=var[:], in0=var[:], scalar1=0.0, scalar2=None, op0=ALU.max), var.k(), var.k())
        P.act(lambda e: e.activation(out=var[:], in_=var[:], func=AF.Ln, bias=epsc[:, 0:1], scale=1.0), var.k() + ["epsc"], var.k())
        P.act(lambda e: e.activation(out=var[:], in_=var[:], func=AF.Exp, scale=-0.5), var.k(), var.k())
        yb = P.get("yb")
        for c in range(2):
            P.dve(lambda e, c=c: e.tensor_tensor(out=cvf[:, c, :], in0=cvf[:, c, :], in1=mean[:], op=ALU.subtract), cvf.k(c) + mean.k(), cvf.k(c))
            P.dve(lambda e, c=c: e.tensor_tensor(out=cvf[:, c, :], in0=cvf[:, c, :], in1=var[:], op=ALU.mult), cvf.k(c) + var.k(), cvf.k(c))
            P.dve(lambda e, c=c: e.tensor_scalar(out=cvf[:, c, :], in0=cvf[:, c, :], scalar1=vcol(("lng", l), c), scalar2=vcol(("lnb", l), c),
                                                 op0=ALU.mult, op1=ALU.add), cvf.k(c) + ["vecs"], cvf.k(c))
            tt = P.get("st")
            P.act(lambda e, c=c, tt=tt: e.activation(out=tt[:], in_=cvf[:, c, :], func=AF.Tanh, scale=0.5), cvf.k(c), tt.k())
            P.dve(lambda e, c=c, tt=tt: e.scalar_tensor_tensor(out=yb[:, c, :], in0=tt[:], scalar=1.0, in1=cvf[:, c, :], op0=ALU.add, op1=ALU.mult),
                  tt.k() + cvf.k(c), yb.k(c))
        return yb

    def conv_part2(l, acc, tg, yb):
        def evac(fc, ps, psk):
            tt = P.get("st")
            P.dve(lambda e: e.scalar_tensor_tensor(out=tt[:], in0=tg[:, fc, :], scalar=1.0, in1=ps, op0=ALU.add, op1=ALU.mult),
                  tg.k(fc) + psk, tt.k())
            P.act(lambda e: e.activation(out=acc[:, fc, :], in_=tt[:], func=AF.Copy, scale=0.5), tt.k(), acc.k(fc))

        stream_proj("cwo", l, list(range(8)), 2, lambda kc: yb[:, kc, :], lambda kc: yb.k(kc), T, evac)

    def gates_proj(l, hT, tg, base):
        def evac(fc, ps, psk):
            P.act(lambda e: e.activation(out=tg[:, fc - base, :], in_=ps, func=AF.Tanh, scale=0.5), psk, tg.k(fc - base))
        stream_proj("winb", l, list(range(base, base + 8)), 8, lambda kc: hT[:, kc, :], lambda kc: hT.k(kc), T, evac)

    def run_pipeline(iters, depth=1, lag=1):
        n = len(iters)
        if n == 0:
            return
        for j in range(min(depth, n)):
            iters[j]["s1"]()
        done = 0
        for k in range(n):
            if k + depth < n:
                iters[k + depth]["s1"]()
            if k >= lag:
                iters[k - lag]["s2b"]()
                done = k - lag + 1
            iters[k]["s2a"]()
        for j in range(done, n):
            iters[j]["s2b"]()

    PO = (psb[3][:, 0, :], [("ps", 6)])
    PD = (psb[3][:, 1, :], [("ps", 7)])

    def win_branch(l, i, hT, acc, tg):
        bq = P.get("bq")

        def evq(fc, ps, psk):
            copy_to(alt(), bq[:, fc, :], ps, psk, bq.k(fc), scale=0.125)
        stream_proj("winb", l, [0, 1, 2, 3], 8, lambda kc: hT[:, kc, :], lambda kc: hT.k(kc), T, evq)
        kh = P.get("kh")
        vlo = P.get("vlo")
        vhi = P.get("vhi")
        lo = i * T - 128
        hi = (i + 1) * T + 128
        a0, a1 = max(lo, 0), min(hi, S)
        tl = list(range(max(i - 1, 0), min(i + 1, NT - 1) + 1))
        if lo < 0:
            P.dve(lambda e: e.memset(kh[:, 0:128], 0.0), [], kh.k())
            P.dve(lambda e: e.memset(vlo[:, 0, :], 0.0), [], vlo.k())
            P.dve(lambda e: e.memset(vhi[:, 0, :], 0.0), [], vhi.k())
        if hi > S:
            P.dve(lambda e: e.memset(kh[:, 640:768], 0.0), [], kh.k())
            P.dve(lambda e: e.memset(vlo[:, 5, :], 0.0), [], vlo.k())
            P.dve(lambda e: e.memset(vhi[:, 5, :], 0.0), [], vhi.k())
        P.dma("sp", kh[:, a0 - lo:a1 - lo], kTs[l][:, a0:a1], reads=[("kTs", l, j) for j in tl], writes=kh.k())
        c0, c1 = (a0 - lo) // 128, (a1 - lo) // 128
        vsrc = vs[l][a0:a1].rearrange("(s p) d -> p s d", p=128)
        P.dma("sp", vlo[:, c0:c1, 0:64], vsrc[:, :, 0:64], reads=[("vs", l, j) for j in tl], writes=vlo.k())
        P.dma("sp", vhi[:, c0:c1, 64:128], vsrc[:, :, 64:128], reads=[("vs", l, j) for j in tl], writes=vhi.k())
        yb = P.get("yb")
        ppt = P.pools["pp"][0]
        ptt = P.pools["ptb"][0]
        iters = []
        kk = 0
        for qb in range(4):
            po, pok = (PO, PD)[qb % 2]
            for c in range(4):
                for hh in range(2):
                    st = {}
                    slot = kk % 4
                    kk += 1

                    def s1(qb=qb, c=c, hh=hh, st=st, slot=slot):
                        g = 4 * i + qb
                        edge = (g == 0) or (g == S // 128 - 1)
                        sc, sck = psb[slot // 2][:, slot % 2, :], [("ps", slot)]
                        P.pe(lambda e: e.matmul(sc[:, 0:384], lhsT=bq[hh * 64:(hh + 1) * 64, c, qb * 128:(qb + 1) * 128],
                                                rhs=kh[hh * 64:(hh + 1) * 64, qb * 128:qb * 128 + 384], start=True, stop=False),
                             bq.k(c) + kh.k(), sck)
                        P.pe(lambda e: e.matmul(sc[:, 0:384], lhsT=identb[:], rhs=t5tab[:, 2 * c + hh, :],
                                                start=False, stop=(not edge)), ["identb", "t5tab"], sck)
                        if edge:
                            v = 0 if g == 0 else 1
                            P.pe(lambda e, v=v: e.matmul(sc[:, 0:384], lhsT=onesb[0:1, :], rhs=cmask[0:1, v, :], start=False, stop=True),
                                 ["onesb", "cmask"], sck)
                        P.pe(lambda e: e.matmul(sc[:, 384:385], lhsT=identb[:], rhs=sinkb[:, 8 * l + 2 * c + hh:8 * l + 2 * c + hh + 1],
                                                start=True, stop=True), ["identb", "sinkb"], sck)
                        sm = P.get("sm")
                        ptile = ppt[(slot // 2) % 2]
                        pps = ptile[:, (slot % 2) * 512:(slot % 2) * 512 + 385]
                        ppk = ptile.k(slot % 2)
                        P.dve(lambda e: e.tensor_reduce(out=sm[:, 0:1], in_=sc[:, 0:385], axis=AX.X, op=ALU.max, negate=True), sck, sm.k())
                        P.act(lambda e: e.activation(out=pps, in_=sc[:, 0:385], func=AF.Exp, bias=sm[:, 0:1], scale=1.0, accum_out=sm[:, 8:9]),
                              sck + sm.k(), ppk + sm.k())
                        P.dve(lambda e: e.reciprocal(out=sm[:, 16:17], in_=sm[:, 8:9]), sm.k(), sm.k())
                        P.dve(lambda e: e.tensor_scalar(out=pps, in0=pps, scalar1=sm[:, 16:17], scalar2=None, op0=ALU.mult), ppk + sm.k(), ppk)
                        st["pps"] = pps
                        st["ppk"] = ppk

                    def s2a(st=st, slot=slot):
                        pps, ppk = st["pps"], st["ppk"]
                        pt, ptk = psb[2][:, slot % 2, :], [("ps", 4 + slot % 2)]
                        for s3 in range(3):
                            P.pe(lambda e, s3=s3: e.matmul(pt[:, s3 * 128:(s3 + 1) * 128], lhsT=pps[:, s3 * 128:(s3 + 1) * 128],
                                                           rhs=identb[:], start=True, stop=True), ppk + ["identb"], ptk)
                        ttile = ptt[(slot // 2) % 2]
                        pbs = ttile[:, (slot % 2) * 512:(slot % 2) * 512 + 384]
                        pbk = ttile.k(slot % 2)
                        copy_to("act", pbs, pt[:, 0:384], ptk, pbk)
                        st["pbs"] = pbs
                        st["pbk"] = pbk

                    def s2b(qb=qb, c=c, hh=hh, st=st, po=po, pok=pok):
                        pbs, pbk = st["pbs"], st["pbk"]
                        vv = vlo if hh == 0 else vhi
                        for s3 in range(3):
                            P.pe(lambda e, s3=s3: e.matmul(
                                po[:, c * 128:(c + 1) * 128], lhsT=vv[:, qb + s3, :], rhs=pbs[:, s3 * 128:(s3 + 1) * 128],
                                start=(hh == 0 and s3 == 0), stop=(hh == 1 and s3 == 2)), vv.k() + pbk, pok)
                        if c == 3 and hh == 1:
                            copy_to(alt(), yb[:, 0:4, qb * 128:(qb + 1) * 128], po.rearrange("p (c q) -> p c q", c=4), pok, yb.k())

                    iters.append({"s1": s1, "s2a": s2a, "s2b": s2b})
        run_pipeline(iters, depth=3, lag=2)
        gates_proj(l, hT, tg, 6 + 8)

        def evac(fc, ps, psk):
            tt = P.get("st")
            P.dve(lambda e: e.scalar_tensor_tensor(out=tt[:], in0=tg[:, fc, :], scalar=1.0, in1=ps, op0=ALU.add, op1=ALU.mult),
                  tg.k(fc) + psk, tt.k())
            P.dve(lambda e: e.tensor_tensor(out=acc[:, fc, :], in0=acc[:, fc, :], in1=tt[:], op=ALU.add), tt.k() + acc.k(fc), acc.k(fc))

        pj_state["set"] = [2, 3, 4, 5]
        stream_proj("wwo", l, list(range(8)), 4, lambda kc: yb[:, kc, :], lambda kc: yb.k(kc), T, evac)
        pj_state["set"] = list(range(8))

    def na_branch(l, i, hT, acc, tg, ybf):
        cq = P.get("cq")

        def evq(fc, ps, psk):
            copy_to(alt(), cq[:, fc - 4, :], ps, psk, cq.k(fc - 4), scale=0.125)
        stream_proj("winb", l, [4, 5], 8, lambda kc: hT[:, kc, :], lambda kc: hT.k(kc), T, evq)
        yb = P.get("yb")
        kvs = {}

        def load_kv(qb):
            j = 4 * i + qb
            kb = min(max(2 * j - 4, 0), 55)
            t0 = kb * 64
            tl = list(range(t0 // T, (t0 + 575) // T + 1))
            ckh = P.get("ckh")
            P.dma("sp", ckh[:], ckTs[l].rearrange("(c p) t -> p c t", p=128)[:, :, t0:t0 + 576],
                  reads=[("ckTs", l, x) for x in tl], writes=ckh.k())
            cvp = P.get("cvp")
            P.dma("sp", cvp[:, 0:4, :, :], cvs[l][t0:t0 + 512, :].rearrange("(s p) (h d) -> p s h d", p=128, h=4),
                  reads=[("cvs", l, x) for x in tl], writes=cvp.k())
            P.dma("sp", cvp[0:64, 4, :, :], cvs[l][t0 + 512:t0 + 576, :].rearrange("p (h d) -> p h d", h=4),
                  reads=[("cvs", l, x) for x in tl], writes=cvp.k())
            v = na_variant(j)
            if v == 2:
                nat, natk = natreg, ["natreg"]
            else:
                nt_ = P.get("nate")
                P.dma("sp", nt_[:], natb[l, v].rearrange("p (h k) -> p h k", h=4), reads=[("natb", l, v)], writes=nt_.k())
                nat, natk = nt_, nt_.k()
            kvs[qb] = dict(ckh=ckh, cvp=cvp, nat=nat, natk=natk)

        load_kv(0)
        iters = []
        for qb in range(4):
            po, pok = (PO, PD)[qb % 2]
            for pr in range(2):
                for hh in range(2):
                    st = {}

                    def s1(qb=qb, pr=pr, hh=hh, st=st):
                        qst = kvs[qb]
                        ckh, nat, natk = qst["ckh"], qst["nat"], qst["natk"]
                        h = 2 * pr + hh
                        sc, sck = sc2()
                        for part in range(2):
                            P.pe(lambda e, part=part: e.matmul(
                                sc[:, part, 0:288], lhsT=cq[hh * 64:(hh + 1) * 64, pr, qb * 128:(qb + 1) * 128],
                                rhs=ckh[hh * 64:(hh + 1) * 64, pr, part * 288:(part + 1) * 288], start=True, stop=False),
                                cq.k(pr) + ckh.k(), sck)
                        for part in range(2):
                            P.pe(lambda e, part=part: e.matmul(
                                sc[:, part, 0:288], lhsT=identb[:], rhs=nat[:, h, part * 288:(part + 1) * 288], start=False, stop=True),
                                ["identb"] + natk, sck)
                        pp = P.get("pp")
                        sm = P.get("sm")
                        P.dve(lambda e: e.tensor_reduce(out=sm[:, 0:1], in_=sc[:, :, 0:288], axis=AX.XY, op=ALU.max, negate=True), sck, sm.k())
                        P.act(lambda e: e.activation(out=pp[:, 0:576].rearrange("p (a b) -> p a b", a=2), in_=sc[:, :, 0:288], func=AF.Exp,
                                                     bias=sm[:, 0:1], scale=1.0, accum_out=sm[:, 8:9]), sck + sm.k(), pp.k() + sm.k())
                        P.dve(lambda e: e.reciprocal(out=sm[:, 16:17], in_=sm[:, 8:9]), sm.k(), sm.k())
                        P.dve(lambda e: e.tensor_scalar(out=pp[:, 0:576], in0=pp[:, 0:576], scalar1=sm[:, 16:17], scalar2=None, op0=ALU.mult),
                              pp.k() + sm.k(), pp.k())
                        st["pp"] = pp
                        if pr == 1 and hh == 0 and qb + 1 < 4:
                            load_kv(qb + 1)

                    def s2a(st=st):
                        pp = st["pp"]
                        pt, ptk = pt2()
                        for s5 in range(5):
                            m = 128 if s5 < 4 else 64
                            dst = pt[0:m, 0, s5 * 128:(s5 + 1) * 128] if s5 < 4 else pt[0:64, 1, 0:128]
                            P.pe(lambda e, s5=s5, m=m, dst=dst: e.matmul(dst, lhsT=pp[:, s5 * 128:s5 * 128 + m], rhs=identb[:], start=True, stop=True),
                                 pp.k() + ["identb"], ptk)
                        ptb = P.get("ptb")
                        copy_to(alt(), ptb[:, 0:512], pt[:, 0, :], ptk, ptb.k())
                        copy_to(alt(), ptb[0:64, 512:640], pt[0:64, 1, 0:128], ptk, ptb.k())
                        st["ptb"] = ptb

                    def s2b(qb=qb, pr=pr, hh=hh, st=st, po=po, pok=pok):
                        ptb = st["ptb"]
                        cvp = kvs[qb]["cvp"]
                        for s5 in range(5):
                            m = 128 if s5 < 4 else 64
                            n = hh * 5 + s5
                            P.pe(lambda e, s5=s5, m=m, n=n: e.matmul(
                                po[:, pr * 128:(pr + 1) * 128], lhsT=cvp[0:m, s5, 2 * pr + hh, :], rhs=ptb[0:m, s5 * 128:(s5 + 1) * 128],
                                start=(n == 0), stop=(n == 9)), cvp.k() + ptb.k(), pok)
                        if pr == 1 and hh == 1:
                            copy_to(alt(), yb[:, 0:2, qb * 128:(qb + 1) * 128], po[:, 0:256].rearrange("p (c q) -> p c q", c=2), pok, yb.k())

                    iters.append({"s1": s1, "s2a": s2a, "s2b": s2b})
        run_pipeline(iters)
        gates_proj(l, hT, tg, 6 + 16)

        def evac(fc, ps, psk):
            tt = P.get("st")
            P.dve(lambda e: e.scalar_tensor_tensor(out=tt[:], in0=tg[:, fc, :], scalar=1.0, in1=ps, op0=ALU.add, op1=ALU.mult),
                  tg.k(fc) + psk, tt.k())
            P.dve(lambda e: e.tensor_tensor(out=ybf[:, fc, :], in0=acc[:, fc, :], in1=tt[:], op=ALU.add), tt.k() + acc.k(fc), ybf.k(fc))

        pj_state["set"] = [2, 3, 4, 5]
        stream_proj("nwo", l, list(range(8)), 2, lambda kc: yb[:, kc, :], lambda kc: yb.k(kc), T, evac)
        pj_state["set"] = list(range(8))

    def cross_block(l, xt):
        hT = P.get("hT")
        rmsnorm(lambda c: xt[:, c, :], lambda c: xt.k(c), ("cross", l), lambda c: hT[:, c, :], lambda c: hT.k(c), T)
        xq_ = P.get("tg")
        xo_ = P.get("ybf")

        def evq(fc, ps, psk):
            copy_to(alt(), xq_[:, fc, :], ps, psk, xq_.k(fc), scale=0.0625)
        pj_state["set"] = [2, 3, 4, 5, 6, 7]
        stream_proj("xq", l, list(range(8)), 8, lambda kc: hT[:, kc, :], lambda kc: hT.k(kc), T, evq)
        ppt = P.pools["pp"][0]
        ptt = P.pools["ptb"][0]
        iters = []
        kk_ = 0
        for qb in range(4):
            for half in range(2):
                st = {}
                slot = kk_ % 4
                kk_ += 1
                po, pok = (PO, PD)[half]

                def s1(qb=qb, half=half, st=st, slot=slot):
                    qs = slice(qb * 128, (qb + 1) * 128)
                    sc, sck = psb[slot // 2][:, slot % 2, :], [("ps", slot)]
                    for hq in range(2):
                        h = 2 * half + hq
                        for kk in range(2):
                            P.pe(lambda e, h=h, hq=hq, kk=kk: e.matmul(sc[:, hq * 256:(hq + 1) * 256], lhsT=xq_[:, 2 * h + kk, qs],
                                                                      rhs=KmT[:, 2 * h + kk, :], start=(kk == 0), stop=(kk == 1)),
                                 xq_.k(2 * h + kk) + ["KmT"], sck)
                    sm = P.get("sm")
                    ptile = ppt[(slot // 2) % 2]
                    pps = ptile[:, (slot % 2) * 512:(slot % 2 + 1) * 512]
                    ppk = ptile.k(slot % 2)
                    P.dve(lambda e: e.tensor_reduce(out=sm[:, 0:1], in_=sc, axis=AX.X, op=ALU.max, negate=True), sck, sm.k())
                    for hq in range(2):
                        P.act(lambda e, hq=hq: e.activation(out=pps[:, hq * 256:(hq + 1) * 256], in_=sc[:, hq * 256:(hq + 1) * 256], func=AF.Exp,
                                                            bias=sm[:, 0:1], scale=1.0, accum_out=sm[:, 8 + hq:9 + hq]), sck + sm.k(), ppk + sm.k())
                    P.dve(lambda e: e.reciprocal(out=sm[:, 16:18], in_=sm[:, 8:10]), sm.k(), sm.k())
                    ppv = pps.rearrange("p (a b) -> p a b", a=2)
                    P.dve(lambda e: e.tensor_tensor(out=ppv, in0=ppv, in1=sm[:, 16:18].unsqueeze(2).broadcast_to([128, 2, 256]), op=ALU.mult),
                          ppk + sm.k(), ppk)
                    st["pps"] = pps
                    st["ppk"] = ppk

                def s2a(st=st, slot=slot):
                    pps, ppk = st["pps"], st["ppk"]
                    pt, ptk = psb[2][:, slot % 2, :], [("ps", 4 + slot % 2)]
                    for j in range(4):
                        P.pe(lambda e, j=j: e.matmul(pt[:, j * 128:(j + 1) * 128], lhsT=pps[:, j * 128:(j + 1) * 128], rhs=identb[:],
                                                     start=True, stop=True), ppk + ["identb"], ptk)
                    ttile = ptt[(slot // 2) % 2]
                    pbs = ttile[:, (slot % 2) * 512:(slot % 2 + 1) * 512]
                    pbk = ttile.k(slot % 2)
                    copy_to("act", pbs, pt, ptk, pbk)
                    st["pbs"] = pbs
                    st["pbk"] = pbk

                def s2b(qb=qb, half=half, st=st, po=po, pok=pok):
                    qs = slice(qb * 128, (qb + 1) * 128)
                    pbs, pbk = st["pbs"], st["pbk"]
                    for hq in range(2):
                        h = 2 * half + hq
                        for dd in range(2):
                            for mc in range(2):
                                P.pe(lambda e, h=h, hq=hq, dd=dd, mc=mc: e.matmul(
                                    po[:, (hq * 2 + dd) * 128:(hq * 2 + dd + 1) * 128],
                                    lhsT=Vm[:, mc, h * 256 + dd * 128:h * 256 + (dd + 1) * 128],
                                    rhs=pbs[:, (hq * 2 + mc) * 128:(hq * 2 + mc + 1) * 128], start=(mc == 0), stop=(mc == 1)),
                                    ["Vm"] + pbk, pok)
                    copy_to(alt(), xo_[:, 4 * half:4 * half + 4, qs], po.rearrange("p (c q) -> p c q", c=4), pok,
                            [k for c in range(4 * half, 4 * half + 4) for k in xo_.k(c)])

                iters.append({"s1": s1, "s2a": s2a, "s2b": s2b})
        run_pipeline(iters, depth=3, lag=2)
        pj_state["set"] = list(range(8))

        def evac(fc, ps, psk):
            P.dve(lambda e: e.tensor_tensor(out=xt[:, fc, :], in0=xt[:, fc, :], in1=ps, op=ALU.add), psk + xt.k(fc), xt.k(fc))
        stream_proj("xo", l, list(range(8)), 8, lambda kc: xo_[:, kc, :], lambda kc: xo_.k(kc), T, evac)

    def mem_prep(l):
        xm = P.get("acc")
        P.dma("sp", xm[:, :, 0:MEM], memT_in.rearrange("(c p) m -> p c m", p=128), writes=xm.k())
        hm = P.get("ybf")
        rmsnorm(lambda c: xm[:, c, 0:MEM], lambda c: xm.k(c), ("mem", l), lambda c: hm[:, c, 0:MEM], lambda c: hm.k(c), MEM)

        def evk(fc, ps, psk):
            copy_to(alt(), KmT[:, fc, :], ps[:, 0:MEM], psk, ["KmT"])
        stream_proj("xk", l, list(range(8)), 8, lambda kc: hm[:, kc, 0:MEM], lambda kc: hm.k(kc), MEM, evk)
        for half in range(2):
            wt = [P.get("w"), P.get("w")]
            for q in range(2):
                src = wb[("xv", l)].rearrange("p (k n) -> p k n", k=8)[:, 4 * q:4 * q + 4, half * 512:(half + 1) * 512]
                P.dma("sp", wt[q][:, 0:2048].rearrange("p (k n) -> p k n", k=4), src, reads=wkeys_flat("xv", l), writes=wt[q].k())
            for mc in range(2):
                ps, psk = pj()
                for kc in range(8):
                    w_ = wt[kc // 4]
                    P.pe(lambda e, kc=kc, mc=mc, ps=ps, w_=w_: e.matmul(ps, lhsT=hm[:, kc, mc * 128:(mc + 1) * 128],
                                                                       rhs=w_[:, (kc % 4) * 512:(kc % 4 + 1) * 512], start=(kc == 0), stop=(kc == 7)),
                         hm.k(kc) + w_.k(), psk)
                copy_to(alt(), Vm[:, mc, half * 512:(half + 1) * 512], ps, psk, ["Vm"])

    def layer_prep(l):
        for c in range(2):
            for j in range(31):
                P.dve(lambda e, c=c, j=j: e.tensor_scalar(out=diag[:, c * 31 + j, :], in0=identf[:], scalar1=vcol(("dw", l), c * 31 + j),
                                                          scalar2=None, op0=ALU.mult), ["identf", "vecs"], ["diag"])
        P.dma("sp", natreg[:], natb[l, 2].rearrange("p (h k) -> p h k", h=4), reads=[("natb", l, 2)], writes=["natreg"])
        mem_prep(l)

    def pass_b(l, i, xt):
        hT = P.get("hT")
        P.dma("sp", hT[:], fm(hTs[l], i), reads=[("hTs", l, i)], writes=hT.k())
        yb1 = conv_part1(l, i)
        acc = P.get("acc")
        tg = P.get("tg")
        ybf = P.get("ybf")
        gates_proj(l, hT, tg, 6)
        conv_part2(l, acc, tg, yb1)
        if stage >= 4:
            win_branch(l, i, hT, acc, tg)
        if stage >= 5:
            na_branch(l, i, hT, acc, tg, ybf)
        else:
            for fc in range(8):
                P.dve(lambda e, fc=fc: e.tensor_copy(out=ybf[:, fc, :], in_=acc[:, fc, :]), acc.k(fc), ybf.k(fc))

        def evac(fc, ps, psk):
            P.dve(lambda e: e.scalar_tensor_tensor(out=xt[:, fc, :], in0=ps, scalar=0.5, in1=xt[:, fc, :], op0=ALU.mult, op1=ALU.add),
                  psk + xt.k(fc), xt.k(fc))
        stream_proj("wo", l, list(range(8)), 8, lambda kc: ybf[:, kc, :], lambda kc: ybf.k(kc), T, evac)
        if stage >= 6:
            cross_block(l, xt)
        if stage >= 7:
            if l == 0:
                flush("late")
            ffn(l, 2, xt)

    def final_norm(i, xt):
        acc = P.get("acc")
        rmsnorm(lambda c: xt[:, c, :], lambda c: xt.k(c), "final", lambda c: acc[:, c, :], lambda c: acc.k(c), T)
        P.dma("pool", fm(outT, i), acc[:], reads=acc.k(), writes=[("out", i)])

    first = ["gu1", "wd1", "winab", "winaa"]
    rest = ["winb", "cwo", "wwo", "nwo", "wo", "xk", "xv", "xq", "xo", "gu2", "wd2"]
    for n in first:
        convert(n, 0)
    P.dma("pool", t5tab[:], t5_in.rearrange("p (h s) -> p h s", h=8), writes=["t5tab"])
    P.dma("pool", cmask[:], cmask_in.rearrange("p (v s) -> p v s", v=2), writes=["cmask"])

    def dump(i, xt):
        P.dma("sp", fm(outT, i), xt[:], reads=xt.k(), writes=[("out", i)])

    def load_x(src, i, rk):
        xt = P.get("xT")
        P.dma("sp", xt[:], fm(src, i), reads=rk, writes=xt.k())
        return xt

    early = ["xk", "xv", "winb", "cwo", "wwo", "nwo", "wo", "xq", "xo"]
    late = ["gu2", "wd2"]
    listA = []
    if stage >= 2:
        for l_ in range(L):
            for v_ in range(5):
                listA.append(("natb", lambda gate, l_=l_, v_=v_: P.dma("pool", natb[l_, v_], natab_in[l_, v_], reads=gate,
                                                                        writes=[("natb", l_, v_)], ring="conv")))
        for n in early:
            listA += [("early", it) for it in convert_items(n, 0)]
        for n in late:
            listA += [("late", it) for it in convert_items(n, 0)]
        if stage >= 8:
            for n in first:
                listA += [("first1", it) for it in convert_items(n, 1)]
            for n in early + late:
                listA += [("rest1", it) for it in convert_items(n, 1)]

    def emit_some(lst, k):
        gate = tick()
        for _ in range(min(k, len(lst))):
            lst.pop(0)(gate)

    CQ = {"lst": listA, "rate": 0.0, "acc": 0.0}

    def conv_point():
        CQ["acc"] += CQ["rate"]
        while CQ["acc"] >= 1.0 and CQ["lst"]:
            CQ["acc"] -= 1.0
            CQ["lst"].pop(0)[1](tick())
        if not CQ["lst"]:
            CQ["acc"] = 0.0

    def flush(tag):
        lst = CQ["lst"]
        while any(t == tag for t, _ in lst):
            lst.pop(0)[1](tick())

    nxt = load_x(xT_in, 0, [])
    for i in range(NT):
        xt = nxt
        if i == 1:
            HOOKS["conv_point"] = conv_point
            CQ["rate"] = 0.6
        hook = (lambda i=i: load_x(xT_in, i + 1, [])) if i + 1 < NT else None
        res_ = pass_a(0, i, xt, hook)
        if res_ is not None:
            nxt = res_
        if stage == 1:
            dump(i, xt)
        else:
            P.dma("pool", fm(xs, i), xt[:], reads=xt.k(), writes=[("xs", i)])
    for l in range(L):
        if stage < 2:
            break
        flush("early" if l == 0 else "rest1")
        layer_prep(l)
        nxt = load_x(xs, 0, [("xs", 0)])
        for i in range(NT):
            xt = nxt
            if l == 0 and i == 0:
                CQ["rate"] = 0.25
            pass_b(l, i, xt)
            if i + 1 < NT:
                nxt = load_x(xs, i + 1, [("xs", i + 1)])
            if stage < 8:
                dump(i, xt)
                continue
            if l + 1 < L:
                flush("first1")
                pass_a(l + 1, i, xt)
                P.dma("pool", fm(xs, i), xt[:], reads=xt.k(), writes=[("xs", i)])
            else:
                final_norm(i, xt)
        if stage < 8:
            break
    P.add("sp", lambda e: e.nop(), reads=[("out", i) for i in range(NT)])
    P.emit()
    P.close()
    return nc


_CACHE = {}


def kernel(stage=99, cores=8, **inputs):
    inp = {k: np.asarray(v) for k, v in inputs.items()}
    shared = prep_weights(inp)
    in_maps = []
    for b in range(cores):
        m = dict(shared)
        m["xT"] = np.ascontiguousarray(inp["x"][b].T)
        m["memT"] = np.ascontiguousarray(inp["mem"][b].T)
        in_maps.append(m)
    if stage not in _CACHE:
        _CACHE[stage] = build_program(stage)
    nc = _CACHE[stage]
    res = run_bass_kernel_spmd(nc, in_maps, core_ids=list(range(cores)))
    out = np.stack([np.ascontiguousarray(r["outT"].T) for r in res.results], axis=0)
    return out.astype(np.float32)
```
